# Optimizing a Trainium2 kernel written in Bass

```python
import math
import jax, jax.numpy as jnp
from jax import lax
import numpy as np

D_MODEL = 1024
BATCH = 8
SEQ = 2048
DEPTH = 2

GRID_W = 64
CTX_LEN = 256
MLP_HIDDEN = 4 * D_MODEL
ADA_CHUNKS = 6
SHORT_CONV = 3
EPS = 1e-6

RW_WIDTH = D_MODEL // 2
RW_HEAD_DIM = 64
RW_HEADS = RW_WIDTH // RW_HEAD_DIM
RW_DECAY_RANK = 64
RW_AAA_RANK = 64
RW_GATE_RANK = 128
RW_COLS = 3 * RW_WIDTH + RW_DECAY_RANK + RW_AAA_RANK + RW_GATE_RANK
RW_SPLITS = (RW_WIDTH, 2 * RW_WIDTH, 3 * RW_WIDTH, 3 * RW_WIDTH + RW_DECAY_RANK,
             3 * RW_WIDTH + RW_DECAY_RANK + RW_AAA_RANK)
RW_GN_EPS = 64e-5
HY_WIDTH = D_MODEL - RW_WIDTH
HY_COLS = 3 * HY_WIDTH
HY_BANDS = 16
HY_EMB = 1 + 2 * HY_BANDS
HY_ORDER = 64
HY_FAST_DECAY = 0.3
HY_SLOW_DECAY = 1.5
HY_TARGET = 1e-2
AB_COLS = RW_COLS + HY_COLS
DN_HEADS = 8
DN_HEAD_DIM = D_MODEL // DN_HEADS
DN_DIM = DN_HEADS * DN_HEAD_DIM
DN_CHUNK = 64
DN_COLS = 4 * DN_DIM + 4 * DN_HEADS

kernel_name = 'hybrid_rwkv7_hyena_gdn_diffusion_trunk'


def rmsnorm(x, w):
    xf = x.astype(jnp.float32)
    y = xf * lax.rsqrt(jnp.mean(xf * xf, axis=-1, keepdims=True) + EPS)
    return y.astype(x.dtype) * w


def modulate(h, shift, scale):
    return h * (1 + scale) + shift


def l2norm(t):
    tf = t.astype(jnp.float32)
    return tf * lax.rsqrt(jnp.sum(tf * tf, axis=-1, keepdims=True) + EPS)


def short_conv(u, w):
    pad = SHORT_CONV // 2
    return lax.conv_general_dilated(u, w[:, None, :], window_strides=(1,), padding=((pad, pad),),
                                    dimension_numbers=('NWC', 'WIO', 'NWC'),
                                    feature_group_count=u.shape[-1])


def token_shift_mix(p, mu):
    prev = jnp.pad(p, ((0, 0), (1, 0), (0, 0)))[:, :-1]
    nxt = jnp.pad(p, ((0, 0), (0, 1), (0, 0)))[:, 1:]
    return p + mu * (0.5 * (prev + nxt) - p)


def rwkv_prep(p, mu, w0, w_up, a0, a_up, g_up, k_k, k_a):
    B, L, _ = p.shape
    p = token_shift_mix(p, mu).astype(jnp.float32)
    r, k, v, wl, al, gl = jnp.split(p, RW_SPLITS, axis=-1)
    heads = lambda t: t.reshape(B, L, RW_HEADS, RW_HEAD_DIM)
    kk = l2norm(heads(k * k_k))
    wl = jnp.tanh(wl)
    dirs = []
    for d in range(2):
        w_log = -jax.nn.softplus(-(w0[d] + wl @ w_up[d])) - 0.5
        a = jax.nn.sigmoid(a0[d] + al @ a_up[d])
        k_d = k * (1 + (a - 1) * k_a)
        dirs.append((heads(jnp.exp(-jnp.exp(w_log))), heads(k_d), heads(a) * kk))
    g = jax.nn.sigmoid(gl) @ g_up
    return heads(r), heads(v), kk, g, dirs


def rwkv_scan(S0, r, v, kk, decay, k, b, reverse):
    xs = tuple(jnp.moveaxis(t, 1, 0) for t in (r, v, kk, decay, k, b))

    def step(S, inp):
        r_t, v_t, kk_t, w_t, k_t, b_t = inp
        sa = -jnp.einsum('bhvk,bhk->bhv', S, kk_t)
        S = S * w_t[:, :, None, :] + sa[..., None] * b_t[:, :, None, :] + v_t[..., None] * k_t[:, :, None, :]
        return S, jnp.einsum('bhvk,bhk->bhv', S, r_t)

    S, ys = lax.scan(step, S0, xs, reverse=reverse)
    return jnp.moveaxis(ys, 0, 1), S


def rwkv_output(y, prep, r_k, ln_w, ln_b):
    r, v, _, g, dirs = prep
    B, L = y.shape[:2]
    mean = jnp.mean(y, axis=-1, keepdims=True)
    var = jnp.mean(jnp.square(y - mean), axis=-1, keepdims=True)
    yn = (y - mean) * lax.rsqrt(var + RW_GN_EPS)
    bonus = sum(jnp.sum(r * k_d * r_k, axis=-1, keepdims=True) for _, k_d, _ in dirs) * v
    return (yn.reshape(B, L, RW_WIDTH) * ln_w + ln_b + bonus.reshape(B, L, RW_WIDTH)) * g


def hyena_filter(L, f_w1, f_b1, f_w2, f_b2, f_w3, f_b3, f_w4, freq):
    t = jnp.linspace(0.0, 1.0, L, dtype=jnp.float32)[:, None]
    w = 2 * math.pi * jnp.arange(L, dtype=jnp.float32)[:, None] / L
    f = jnp.linspace(1e-4, HY_BANDS - 1, HY_BANDS, dtype=jnp.float32)[None, :]
    z = jnp.concatenate([t, jnp.cos(f * w), -jnp.sin(f * w)], axis=-1)
    h = jnp.sin(freq * (z @ f_w1 + f_b1))
    h = jnp.sin(freq * (h @ f_w2 + f_b2))
    h = jnp.sin(freq * (h @ f_w3 + f_b3))
    h = h @ f_w4
    deltas = jnp.abs(jnp.linspace(math.log(HY_TARGET) / HY_SLOW_DECAY, math.log(HY_TARGET) / HY_FAST_DECAY,
                                  HY_WIDTH, dtype=jnp.float32))
    h = h * jnp.exp(-t * jnp.tile(deltas, 2))
    h_f, h_b = jnp.split(h, 2, axis=-1)
    kern = jnp.concatenate([h_f, jnp.zeros((1, HY_WIDTH), h.dtype), h_b[:0:-1]], axis=0)
    return kern / jnp.sum(jnp.abs(kern), axis=0, keepdims=True)


def hyena_seq(u, conv_w, conv_b, skip, filt):
    L = u.shape[1]
    u = short_conv(u, conv_w) + conv_b
    x0, x1, v = jnp.split(u, 3, axis=-1)
    s = (x1 * v).astype(jnp.float32)
    kern = hyena_filter(L, *filt)
    y = jnp.fft.irfft(jnp.fft.rfft(s, n=2 * L, axis=1) * jnp.fft.rfft(kern, axis=0)[None], n=2 * L, axis=1)[:, :L]
    return x0 * (y + s * skip)


def rwkv_hyena_mixer(h_ctx, h_lat, w_in, w_out, rw_in, rw_out, hy_conv, hy_filt, with_ctx_out):
    p_ctx, p_lat = h_ctx @ w_in, h_lat @ w_in
    prep_c = rwkv_prep(p_ctx[..., :RW_COLS], *rw_in)
    prep_l = rwkv_prep(p_lat[..., :RW_COLS], *rw_in)
    S0 = jnp.zeros((h_lat.shape[0], RW_HEADS, RW_HEAD_DIM, RW_HEAD_DIM), jnp.float32)
    y_c, y_l = 0.0, 0.0
    for d, rev in enumerate((False, True)):
        yc_d, S_c = rwkv_scan(S0, prep_c[0], prep_c[1], prep_c[2], *prep_c[4][d], reverse=rev)
        yl_d, _ = rwkv_scan(S_c, prep_l[0], prep_l[1], prep_l[2], *prep_l[4][d], reverse=rev)
        y_c, y_l = y_c + yc_d, y_l + yl_d
    a_lat = rwkv_output(y_l, prep_l, *rw_out)
    b_lat = hyena_seq(p_lat[..., RW_COLS:], *hy_conv, hy_filt)
    y_lat = jnp.concatenate([a_lat, b_lat], axis=-1).astype(h_lat.dtype) @ w_out
    y_ctx = None
    if with_ctx_out:
        a_ctx = rwkv_output(y_c, prep_c, *rw_out)
        b_ctx = hyena_seq(p_ctx[..., RW_COLS:], *hy_conv, hy_filt)
        y_ctx = jnp.concatenate([a_ctx, b_ctx], axis=-1).astype(h_ctx.dtype) @ w_out
    return y_ctx, y_lat


def to_chunks(t):
    B, L, H = t.shape[:3]
    t = t.reshape((B, L // DN_CHUNK, DN_CHUNK, H) + t.shape[3:])
    return jnp.moveaxis(t, 3, 2).swapaxes(0, 1)


def from_chunks(t):
    t = jnp.moveaxis(t.swapaxes(0, 1), 2, 3)
    return t.reshape((t.shape[0], -1) + t.shape[3:])


def gdn_chunked(q, k, v, g, beta, S0):
    K = q.shape[-1]
    q = to_chunks(q.astype(jnp.float32) * K ** -0.5)
    k = to_chunks(k.astype(jnp.float32))
    v = to_chunks(v.astype(jnp.float32))
    g = jnp.cumsum(to_chunks(g.astype(jnp.float32)), axis=-1)
    beta = to_chunks(beta.astype(jnp.float32))
    idx = jnp.arange(DN_CHUNK)
    incl = idx[:, None] >= idx[None, :]
    strict = idx[:, None] > idx[None, :]
    diff = g[..., :, None] - g[..., None, :]
    decay = jnp.where(incl, jnp.exp(jnp.where(incl, diff, 0.0)), 0.0)
    kb = k * beta[..., None]
    lower = jnp.where(strict, jnp.einsum('nbhck,nbhsk->nbhcs', kb, k) * decay, 0.0)
    A = lower + jnp.eye(DN_CHUNK, dtype=jnp.float32)
    solve = lambda rhs: lax.linalg.triangular_solve(A, rhs, left_side=True, lower=True)
    u = solve(v * beta[..., None])
    w = solve(kb * jnp.exp(g)[..., None])
    qk = jnp.einsum('nbhck,nbhsk->nbhcs', q, k) * decay

    def step(S, inp):
        q_c, k_c, u_c, w_c, g_c, qk_c = inp
        v_new = u_c - jnp.einsum('bhck,bhkv->bhcv', w_c, S)
        o = (jnp.einsum('bhck,bhkv->bhcv', q_c * jnp.exp(g_c)[..., None], S)
             + jnp.einsum('bhcs,bhsv->bhcv', qk_c, v_new))
        g_last = g_c[..., -1:]
        S = S * jnp.exp(g_last)[..., None] + jnp.einsum(
            'bhck,bhcv->bhkv', k_c * jnp.exp(g_last - g_c)[..., None], v_new)
        return S, o

    S, o = lax.scan(step, S0, (q, k, u, w, g, qk))
    return from_chunks(o), S


def gdn_prep(p, conv_w, A_log, dt_bias):
    B, L, _ = p.shape
    qkv, z, a, b = jnp.split(p, (3 * DN_DIM, 4 * DN_DIM, 4 * DN_DIM + 2 * DN_HEADS), axis=-1)
    qkv = jax.nn.silu(short_conv(qkv, conv_w))
    q, k, v = [t.reshape(B, L, DN_HEADS, DN_HEAD_DIM) for t in jnp.split(qkv, 3, axis=-1)]
    a = a.reshape(B, L, 2, DN_HEADS).astype(jnp.float32)
    b = b.reshape(B, L, 2, DN_HEADS).astype(jnp.float32)
    g = -jnp.exp(A_log) * jax.nn.softplus(a + dt_bias)
    beta = jax.nn.sigmoid(b)
    return l2norm(q), l2norm(k), v, z, g, beta


def gated_rmsnorm(o, z, w):
    B, L = o.shape[:2]
    on = o * lax.rsqrt(jnp.mean(o * o, axis=-1, keepdims=True) + EPS) * w
    return (on.reshape(B, L, DN_DIM) * jax.nn.silu(z.astype(jnp.float32))).astype(z.dtype)


def deltanet_mixer(h_ctx, h_lat, w_in, conv_w, A_log, dt_bias, norm_w, w_out, with_ctx_out):
    q_c, k_c, v_c, z_c, g_c, b_c = gdn_prep(h_ctx @ w_in, conv_w, A_log, dt_bias)
    q_l, k_l, v_l, z_l, g_l, b_l = gdn_prep(h_lat @ w_in, conv_w, A_log, dt_bias)
    S0 = jnp.zeros((h_lat.shape[0], DN_HEADS, DN_HEAD_DIM, DN_HEAD_DIM), jnp.float32)
    o_c, o_l = 0.0, 0.0
    for d in range(2):
        fl = (lambda t: jnp.flip(t, axis=1)) if d == 1 else (lambda t: t)
        oc_d, S_c = gdn_chunked(fl(q_c), fl(k_c), fl(v_c), fl(g_c[:, :, d]), fl(b_c[:, :, d]), S0)
        ol_d, _ = gdn_chunked(fl(q_l), fl(k_l), fl(v_l), fl(g_l[:, :, d]), fl(b_l[:, :, d]), S_c)
        o_c, o_l = o_c + fl(oc_d), o_l + fl(ol_d)
    y_lat = gated_rmsnorm(o_l, z_l, norm_w) @ w_out
    y_ctx = gated_rmsnorm(o_c, z_c, norm_w) @ w_out if with_ctx_out else None
    return y_ctx, y_lat


def sq_relu_mlp(h, w1, w2):
    return jnp.square(jax.nn.relu(h @ w1)) @ w2


def setup_inputs(seed: int = 0) -> dict:
    key = jax.random.key(seed)
    keys = iter(jax.random.split(key, 48))
    f32 = jnp.float32
    nrm = lambda shape, scale=1.0: jax.random.normal(next(keys), shape, f32) * scale
    uni = lambda shape, lo, hi: jax.random.uniform(next(keys), shape, f32, lo, hi)
    D = D_MODEL
    NE = (DEPTH + 1) // 2
    NO = DEPTH // 2
    dt = jnp.exp(uni((NO, 2, DN_HEADS), math.log(1e-3), math.log(1e-1)))
    return {
        'x': nrm((BATCH, SEQ, D)),
        'c': nrm((BATCH, D)),
        'ctx': nrm((BATCH, CTX_LEN, D)),
        'c_ctx': nrm((D,)),
        'ada_w': nrm((DEPTH, D, ADA_CHUNKS * D), 0.5 * D ** -0.5),
        'ada_b': nrm((DEPTH, ADA_CHUNKS * D), 0.02),
        'norm_mix': 1.0 + nrm((DEPTH, D), 0.02),
        'norm_mlp': 1.0 + nrm((DEPTH, D), 0.02),
        'mlp_w1': nrm((DEPTH, D, MLP_HIDDEN), D ** -0.5),
        'mlp_w2': nrm((DEPTH, MLP_HIDDEN, D), MLP_HIDDEN ** -0.5),
        'final_norm': 1.0 + nrm((D,), 0.02),
        'ab_w_in': nrm((NE, D, AB_COLS), D ** -0.5),
        'ab_w_out': nrm((NE, D, D), D ** -0.5),
        'rw_mu': uni((NE, RW_COLS), 0.0, 1.0),
        'rw_w0': uni((NE, 2, RW_WIDTH), -6.0, -1.0),
        'rw_w_up': nrm((NE, 2, RW_DECAY_RANK, RW_WIDTH), 0.1),
        'rw_a0': nrm((NE, 2, RW_WIDTH), 0.5),
        'rw_a_up': nrm((NE, 2, RW_AAA_RANK, RW_WIDTH), RW_AAA_RANK ** -0.5),
        'rw_g_up': nrm((NE, RW_GATE_RANK, RW_WIDTH), RW_GATE_RANK ** -0.5),
        'rw_k_k': 0.85 + nrm((NE, RW_WIDTH), 0.02),
        'rw_k_a': 1.0 + nrm((NE, RW_WIDTH), 0.02),
        'rw_r_k': nrm((NE, RW_HEADS, RW_HEAD_DIM), 0.1),
        'rw_ln_w': 1.0 + nrm((NE, RW_WIDTH), 0.02),
        'rw_ln_b': nrm((NE, RW_WIDTH), 0.02),
        'hy_conv_w': nrm((NE, SHORT_CONV, HY_COLS), SHORT_CONV ** -0.5),
        'hy_conv_b': nrm((NE, HY_COLS), 0.02),
        'hy_f_w1': nrm((NE, HY_EMB, HY_ORDER), HY_EMB ** -0.5),
        'hy_f_b1': nrm((NE, HY_ORDER), 0.1),
        'hy_f_w2': nrm((NE, HY_ORDER, HY_ORDER), HY_ORDER ** -0.5),
        'hy_f_b2': nrm((NE, HY_ORDER), 0.1),
        'hy_f_w3': nrm((NE, HY_ORDER, HY_ORDER), HY_ORDER ** -0.5),
        'hy_f_b3': nrm((NE, HY_ORDER), 0.1),
        'hy_f_w4': nrm((NE, HY_ORDER, 2 * HY_WIDTH), HY_ORDER ** -0.5),
        'hy_freq': 1.0 + nrm((NE, HY_ORDER), 0.02),
        'hy_skip': nrm((NE, HY_WIDTH)),
        'dn_w_in': nrm((NO, D, DN_COLS), D ** -0.5),
        'dn_conv_w': nrm((NO, SHORT_CONV, 3 * DN_DIM), SHORT_CONV ** -0.5),
        'dn_A_log': jnp.log(uni((NO, 2, DN_HEADS), 1.0, 16.0)),
        'dn_dt_bias': dt + jnp.log(-jnp.expm1(-dt)),
        'dn_norm': 1.0 + nrm((NO, DN_HEAD_DIM), 0.02),
        'dn_w_out': nrm((NO, DN_DIM, D), DN_DIM ** -0.5),
    }


def reference(x, c, ctx, c_ctx, ada_w, ada_b, norm_mix, norm_mlp, mlp_w1, mlp_w2, final_norm,
              ab_w_in, ab_w_out, rw_mu, rw_w0, rw_w_up, rw_a0, rw_a_up, rw_g_up, rw_k_k, rw_k_a,
              rw_r_k, rw_ln_w, rw_ln_b, hy_conv_w, hy_conv_b, hy_f_w1, hy_f_b1, hy_f_w2, hy_f_b2,
              hy_f_w3, hy_f_b3, hy_f_w4, hy_freq, hy_skip, dn_w_in, dn_conv_w, dn_A_log, dn_dt_bias,
              dn_norm, dn_w_out):
    rows = x.shape[1] // GRID_W
    assert rows * GRID_W == x.shape[1]
    x_lat, x_ctx = x, ctx
    for l in range(DEPTH):
        last = l == DEPTH - 1
        mod_lat = jax.nn.silu(c) @ ada_w[l] + ada_b[l]
        mod_ctx = jax.nn.silu(c_ctx) @ ada_w[l] + ada_b[l]
        ml = jnp.split(mod_lat[:, None, :], ADA_CHUNKS, axis=-1)
        mc = jnp.split(mod_ctx, ADA_CHUNKS, axis=-1)
        h_lat = modulate(rmsnorm(x_lat, norm_mix[l]), ml[0], ml[1])
        h_ctx = modulate(rmsnorm(x_ctx, norm_mix[l]), mc[0], mc[1])
        if l % 2 == 0:
            e = l // 2
            y_ctx, y_lat = rwkv_hyena_mixer(
                h_ctx, h_lat, ab_w_in[e], ab_w_out[e],
                (rw_mu[e], rw_w0[e], rw_w_up[e], rw_a0[e], rw_a_up[e], rw_g_up[e], rw_k_k[e], rw_k_a[e]),
                (rw_r_k[e], rw_ln_w[e], rw_ln_b[e]),
                (hy_conv_w[e], hy_conv_b[e], hy_skip[e]),
                (hy_f_w1[e], hy_f_b1[e], hy_f_w2[e], hy_f_b2[e], hy_f_w3[e], hy_f_b3[e], hy_f_w4[e], hy_freq[e]),
                with_ctx_out=not last)
        else:
            o = l // 2
            y_ctx, y_lat = deltanet_mixer(h_ctx, h_lat, dn_w_in[o], dn_conv_w[o], dn_A_log[o], dn_dt_bias[o],
                                          dn_norm[o], dn_w_out[o], with_ctx_out=not last)
        x_lat = x_lat + ml[2] * y_lat
        x_lat = x_lat + ml[5] * sq_relu_mlp(modulate(rmsnorm(x_lat, norm_mlp[l]), ml[3], ml[4]), mlp_w1[l], mlp_w2[l])
        if not last:
            x_ctx = x_ctx + mc[2] * y_ctx
            x_ctx = x_ctx + mc[5] * sq_relu_mlp(modulate(rmsnorm(x_ctx, norm_mlp[l]), mc[3], mc[4]), mlp_w1[l], mlp_w2[l])
    return rmsnorm(x_lat, final_norm)
```

```python
import math
import contextlib
import numpy as np
import ml_dtypes
import concourse.bass as bass
import concourse.mybir as mybir
from concourse.bass_utils import run_bass_kernel_spmd

F32 = mybir.dt.float32
BF16 = mybir.dt.bfloat16
ALU = mybir.AluOpType
AF = mybir.ActivationFunctionType
AX = mybir.AxisListType

ENGS = ("pe", "act", "dve", "pool", "sp")
NDMASEM = 6


class Op:
    __slots__ = ("id", "eng", "fn", "reads", "writes", "is_dma", "idx", "deps", "signal",
                 "ticket", "dslot", "dval", "waits", "dma_prev", "extra", "raw")

    def __init__(self, id, eng, fn, reads, writes, is_dma):
        self.id = id
        self.eng = eng
        self.fn = fn
        self.reads = reads
        self.writes = writes
        self.is_dma = is_dma
        self.signal = False
        self.ticket = None
        self.waits = []
        self.dma_prev = None
        self.extra = ()


class _Rec:
    def __init__(self):
        self.call = None

    def __getattr__(self, name):
        def f(*a, **k):
            self.call = (name, a, k)
            return self
        return f


class Sched:
    def __init__(self, nc):
        self.nc = nc
        self.ops = []
        self.per_eng = {e: [] for e in ENGS}
        self.pending = {e: None for e in ENGS}
        self.dma_since = []

    @staticmethod
    def _k(x):
        if isinstance(x, tuple):
            return tuple(Sched._k(i) for i in x)
        if isinstance(x, (str, int)):
            return x
        return x.name

    def _ks(self, xs):
        return tuple(self._k(x) for x in xs)

    def _add(self, o):
        o.idx = len(self.per_eng[o.eng])
        if self.pending[o.eng] is not None:
            o.extra = self.pending[o.eng]
            self.pending[o.eng] = None
        self.per_eng[o.eng].append(o)
        self.ops.append(o)
        if o.is_dma:
            self.dma_since.append(o)
        return o

    def op(self, eng, fn, reads=(), writes=()):
        rec = _Rec()
        fn(rec)
        name, a, k = rec.call

        def fn2(e, name=name, a=a, k=k):
            return getattr(e, name)(*a, **k)
        return self._add(Op(len(self.ops), eng, fn2, self._ks(reads), self._ks(writes), False))

    def dma(self, eng, out, in_, reads=(), writes=(), **kw):
        def fn(e, out=out, in_=in_, kw=kw):
            return e.dma_start(out=out, in_=in_, **kw)
        return self._add(Op(len(self.ops), eng, fn, self._ks(reads), self._ks(writes), True))

    def barrier(self):
        deps = [self.per_eng[e][-1] for e in ENGS if self.per_eng[e]] + list(self.dma_since)
        self.dma_since = []
        for e in ENGS:
            prev = self.pending[e] or ()
            self.pending[e] = tuple(prev) + tuple(deps)

    def finalize(self, out_ops=()):
        nc = self.nc
        last_w = {}
        readers = {}
        for o in self.ops:
            deps = set(o.extra)
            raw = set()
            for r in o.reads:
                w = last_w.get(r)
                if w is not None:
                    deps.add(w)
                    raw.add(w)
            for w_ in o.writes:
                lw = last_w.get(w_)
                if lw is not None:
                    deps.add(lw)
                for rd in readers.get(w_, ()):
                    deps.add(rd)
            deps.discard(o)
            raw.discard(o)
            o.deps = deps
            o.raw = raw
            for r in o.reads:
                readers.setdefault(r, []).append(o)
            for w_ in o.writes:
                last_w[w_] = o
                readers[w_] = []
        dma_count = {e: 0 for e in ENGS}
        dma_slot_last = {}
        for o in self.ops:
            if o.is_dma:
                k = dma_count[o.eng]
                dma_count[o.eng] += 1
                o.dslot = (o.eng, k % NDMASEM)
                o.dval = 16 * (k // NDMASEM + 1)
                o.dma_prev = dma_slot_last.get(o.dslot)
                dma_slot_last[o.dslot] = o
        seen = {e: {p: -1 for p in ENGS} for e in ENGS}
        seen_dma = {e: {} for e in ENGS}
        for o in self.ops:
            e = o.eng
            need = {}
            dneed = {}
            dl = list(o.deps)
            if o.is_dma and o.dma_prev is not None:
                dl.append(o.dma_prev)
            for d in dl:
                if d.is_dma:
                    if seen_dma[e].get(d.dslot, 0) < d.dval:
                        dneed[d.dslot] = max(dneed.get(d.dslot, 0), d.dval)
                else:
                    if d.eng == e and not o.is_dma and e == "pe":
                        continue
                    if seen[e][d.eng] < d.idx:
                        if d.eng not in need or need[d.eng].idx < d.idx:
                            need[d.eng] = d
            for pe_, d in need.items():
                d.signal = True
                seen[e][pe_] = d.idx
                o.waits.append(("c", d))
            for slot, v in dneed.items():
                seen_dma[e][slot] = v
                o.waits.append(("d", slot, v))
        for e in ENGS:
            t = 0
            for o in self.per_eng[e]:
                if o.signal and not o.is_dma:
                    t += 1
                    o.ticket = t
        self.stats = {e: len(self.per_eng[e]) for e in ENGS}
        with contextlib.ExitStack() as es:
            csem = {e: es.enter_context(nc.semaphore("cs_" + e)) for e in ENGS}
            dsem = {}
            for e in ENGS:
                for s in range(min(NDMASEM, dma_count[e])):
                    dsem[(e, s)] = es.enter_context(nc.semaphore("ds_%s_%d" % (e, s)))
            block = es.enter_context(nc.Block())
            engobj = {"pe": block.tensor, "act": block.scalar, "dve": block.vector,
                      "pool": block.gpsimd, "sp": block.sync}
            final = {}
            for o in out_ops:
                final[o.dslot] = max(final.get(o.dslot, 0), o.dval)

            def make(e):
                lst = self.per_eng[e]

                def body(eng):
                    for o in lst:
                        for w in o.waits:
                            if w[0] == "c":
                                eng.wait_ge(csem[w[1].eng], w[1].ticket)
                            else:
                                eng.wait_ge(dsem[w[1]], w[2])
                        ins = o.fn(eng)
                        if o.is_dma:
                            ins.then_inc(dsem[o.dslot], 16)
                        elif o.signal:
                            ins.then_inc(csem[e], 1)
                    if e == "sp":
                        for slot, v in final.items():
                            eng.wait_ge(dsem[slot], v)
                return body

            for e in ENGS:
                if self.per_eng[e] or e == "sp":
                    engobj[e](make(e))
        return self.stats


class Cfg:
    def __init__(self, D=1024, TL=2048, TC=256, DNH=8):
        self.D, self.TL, self.TC = D, TL, TC
        self.T = TL + TC
        self.DC = D // 128
        self.NTC, self.NTL = TC // 128, TL // 128
        self.NT = self.NTC + self.NTL
        self.HID = 4 * D
        self.RW = D // 2
        self.RWH = self.RW // 64
        self.RWC = 3 * self.RW + 64 + 64 + 128
        self.HY = D - self.RW
        self.HYC = 3 * self.HY
        self.ABC = self.RWC + self.HYC
        self.DNH = DNH
        self.DND = D // DNH
        self.DNC = 4 * D + 4 * DNH
        self.EPS = 1e-6


def tblocks(cfg, maxn=512):
    out = []
    for (s0, L) in ((0, cfg.TC), (cfg.TC, cfg.TL)):
        o = 0
        while o < L:
            n = min(maxn, L - o)
            out.append((s0 + o, n))
            o += n
    return out


class Builder:
    def __init__(self, cfg, dbg=()):
        self.cfg = cfg
        self.dbg = set(dbg)
        self.nc = bass.Bass("TRN2", target_bir_lowering=False)
        self.S = Sched(self.nc)
        self.es = contextlib.ExitStack()
        self.inp = {}
        self.outs = []
        self.n_dbg = 0
        self.scopes = []
        self.nsb = 0
        self.pk = self.bk = self.tk = self.hk = 0

    def din(self, name, shape, dt=F32):
        t = self.nc.dram_tensor(name, list(shape), dt, kind="ExternalInput").ap()
        self.inp[name] = t
        return t

    def dscratch(self, name, shape, dt=F32):
        return self.nc.dram_tensor(name, list(shape), dt, kind="Internal").ap()

    def dout(self, name, shape, dt=F32):
        return self.nc.dram_tensor(name, list(shape), dt, kind="ExternalOutput").ap()

    def sb(self, name, shape, dt=F32):
        st = self.scopes[-1] if self.scopes else self.es
        self.nsb += 1
        return st.enter_context(self.nc.sbuf_tensor("%s_%d" % (name, self.nsb), list(shape), dt))

    def scope_begin(self):
        self.scopes.append(contextlib.ExitStack())

    def scope_end(self):
        self.S.barrier()
        self.scopes.pop().close()

    def ps(self, name, shape, dt=F32):
        return self.es.enter_context(self.nc.psum_tensor(name, list(shape), dt))

    def dump(self, name, tile_ap, shape, key, dt=F32):
        d = self.dout("dbg_" + name, shape, dt)
        o = self.S.dma("sp", d, tile_ap, reads=[key])
        self.outs.append(o)

    def build(self):
        cfg, S, nc = self.cfg, self.S, self.nc
        D, T, TC, TL, DC, NT = cfg.D, cfg.T, cfg.TC, cfg.TL, cfg.DC, cfg.NT
        A = {}
        A["x"] = self.din("x", [TL, D])
        A["ctx"] = self.din("ctx", [TC, D])
        A["c2"] = self.din("c2", [2, D])
        A["ada_w"] = self.din("ada_w", [2, D, 6 * D])
        A["ada_b"] = self.din("ada_b", [2, 6 * D])
        A["norm_mix"] = self.din("norm_mix", [2, D])
        A["norm_mlp"] = self.din("norm_mlp", [2, D])
        A["mlp_w1"] = self.din("mlp_w1", [2, D, cfg.HID])
        A["mlp_w2"] = self.din("mlp_w2", [2, cfg.HID, D])
        A["final_norm"] = self.din("final_norm", [D])
        RW, HY = cfg.RW, cfg.HY
        A["ab_w_in"] = self.din("ab_w_in", [D, cfg.ABC])
        A["ab_w_out"] = self.din("ab_w_out", [D, D])
        A["rw_mu"] = self.din("rw_mu", [cfg.RWC])
        for nm in ("rw_w0", "rw_a0"):
            A[nm] = self.din(nm, [2, RW])
        for nm in ("rw_w_up", "rw_a_up"):
            A[nm] = self.din(nm, [2, 64, RW])
        A["rw_g_up"] = self.din("rw_g_up", [128, RW])
        for nm in ("rw_k_k", "rw_k_a", "rw_r_k", "rw_ln_w", "rw_ln_b"):
            A[nm] = self.din(nm, [RW])
        A["hy_conv_w"] = self.din("hy_conv_w", [3, cfg.HYC])
        A["hy_conv_b"] = self.din("hy_conv_b", [cfg.HYC])
        A["hy_f_w1"] = self.din("hy_f_w1", [33, 64])
        A["hy_f_w2"] = self.din("hy_f_w2", [64, 64])
        A["hy_f_w3"] = self.din("hy_f_w3", [64, 64])
        A["hy_f_w4"] = self.din("hy_f_w4", [64, 2 * HY])
        for nm in ("hy_f_b1", "hy_f_b2", "hy_f_b3", "hy_freq"):
            A[nm] = self.din(nm, [64])
        A["hy_skip"] = self.din("hy_skip", [HY])
        A["hy_negdelta"] = self.din("hy_negdelta", [2 * HY])
        A["hy_z_ctx"] = self.din("hy_z_ctx", [33, TC])
        A["hy_z_lat"] = self.din("hy_z_lat", [33, TL])
        A["hy_t_ctx"] = self.din("hy_t_ctx", [TC])
        A["hy_t_lat"] = self.din("hy_t_lat", [TL])
        A["dn_w_in"] = self.din("dn_w_in", [D, cfg.DNC])
        A["dn_conv_w"] = self.din("dn_conv_w", [3, 3 * D])
        A["dn_A_log"] = self.din("dn_A_log", [2, cfg.DNH])
        A["dn_dt_bias"] = self.din("dn_dt_bias", [2, cfg.DNH])
        A["dn_norm"] = self.din("dn_norm", [128])
        A["dn_w_out"] = self.din("dn_w_out", [D, D])
        A["blk64"] = self.din("blk64", [128, 128])
        A["selH"] = self.din("selH", [2, 128])
        A["ident"] = self.din("ident", [128, 128])
        A["sel2"] = self.din("sel2", [2, 256])
        self.A = A
        self.out = self.dout("out", [TL, D])
        self.xr = self.dscratch("xr", [T, D])
        self.modrow_d = self.dscratch("modrow_d", [2, 2, 6 * D])

        self.ident = self.sb("ident_t", [128, 128])
        self.sel2 = self.sb("sel2_t", [2, 256])
        self.modcol = self.sb("modcol", [128, 2, 6 * DC, 2])
        self.nwcol = self.sb("nwcol", [128, 2, 2, DC])
        self.fncol_b = self.sb("fn_b", [128, D])
        self.SC = self.sb("SC", [128, 2, 2, 2, DC])
        self.gate_b = self.sb("gate_b", [128, D])
        self.gate_c = self.sb("gate_c", [128, D])
        self.xt = [self.sb("xt%d" % i, [128, D]) for i in range(2)]
        self.junk = self.sb("junk", [128, D])
        self.ss = self.sb("ss", [128, 4])
        self.banks = [self.ps("bank%d" % i, [128, 512]) for i in range(8)]

        S.dma("sp", self.ident[:], A["ident"], writes=[self.ident])
        S.dma("sp", self.sel2[:], A["sel2"], writes=[self.sel2])
        S.dma("sp", self.fncol_b[:], A["final_norm"].partition_broadcast(128), writes=[self.fncol_b])
        S.dma("sp", self.xr[0:TC, :], A["ctx"], writes=["xr"])
        S.dma("pool", self.xr[TC:T, :], A["x"], writes=["xr"])

        self.scope_begin()
        self.mod_phase()
        self.scope_end()
        for l in range(2):
            last = (l == 1)
            if l == 0:
                if "noab" not in self.dbg:
                    self.layer0()
            else:
                if "nodn" not in self.dbg:
                    self.layer1()
            self.scope_begin()
            self.hT = self.sb("hT_m", [128, DC, T], BF16)
            self.mlp_phase(l, last)
            self.scope_end()
        self.final_phase()
        S.finalize(out_ops=self.outs)
        return nc

    def mod_phase(self):
        cfg, S, nc, A = self.cfg, self.S, self.nc, self.A
        D, DC = cfg.D, cfg.DC
        silc = self.sb("silc", [128, DC, 2])
        c2col = self.sb("c2col", [128, DC, 2])
        modrow = self.sb("modrow", [2, 6 * D])
        brow = self.sb("brow", [2, 6 * D])
        awst = [self.sb("awst%d" % i, [128, DC, 512]) for i in range(2)]
        for s in range(2):
            S.dma("sp", c2col[:, :, s], A["c2"][s].rearrange("(c p) -> p c", p=128),
                  writes=[c2col], allow_slow_non_contiguous=True)
        S.op("act", lambda e: e.activation(out=silc[:], in_=c2col[:], func=AF.Silu), reads=[c2col], writes=[silc])
        for l in range(2):
            for w_, src in ((0, "norm_mix"), (1, "norm_mlp")):
                S.dma("sp", self.nwcol[:, l, w_, :], A[src][l].rearrange("(c p) -> p c", p=128),
                      writes=[self.nwcol], allow_slow_non_contiguous=True)
        k = 0
        for l in range(2):
            for s in range(2):
                S.dma("pool", brow[s:s + 1, :], A["ada_b"][l:l + 1, :], writes=[brow])
            nblk = 6 * D // 512
            for nb in range(nblk):
                st = awst[k % 2]
                k += 1
                S.dma("sp", st[:], A["ada_w"][l][:, nb * 512:(nb + 1) * 512].rearrange("(c p) n -> p c n", p=128),
                      writes=[st])
                bank = self.banks[k % 2]
                for c in range(DC):
                    S.op("pe", lambda e, c=c, st=st, bank=bank: e.matmul(
                        bank[0:2, :], lhsT=silc[:, c, :], rhs=st[:, c, :], start=(c == 0), stop=(c == DC - 1)),
                        reads=[silc, st], writes=[bank])
                S.op("dve", lambda e, nb=nb, bank=bank: e.tensor_tensor(
                    out=modrow[:, nb * 512:(nb + 1) * 512], in0=bank[0:2, :], in1=brow[:, nb * 512:(nb + 1) * 512],
                    op=ALU.add), reads=[bank, brow], writes=[modrow])
            S.dma("sp", self.modrow_d[l], modrow[:], reads=[modrow], writes=[("modrow_d", l)])
            nj = 6 * DC
            bank = self.banks[2]
            for j in range(nj):
                S.op("pe", lambda e, j=j, bank=bank: e.matmul(
                    bank[:, 2 * j:2 * j + 2], lhsT=modrow[0:2, j * 128:(j + 1) * 128], rhs=self.ident[0:2, 0:2],
                    start=True, stop=True), reads=[modrow, self.ident], writes=[bank])
            S.op("dve", lambda e, l=l, bank=bank, nj=nj: e.tensor_copy(
                out=self.modcol[:, l, :, :], in_=bank[:, 0:2 * nj].rearrange("p (j s) -> p j s", s=2)),
                reads=[bank], writes=[self.modcol])
        for l in range(2):
            for w_ in range(2):
                for s in range(2):
                    kk = 1 if w_ == 0 else 4
                    S.op("dve", lambda e, l=l, w_=w_, s=s, kk=kk: e.scalar_tensor_tensor(
                        out=self.SC[:, l, w_, s, :], in0=self.modcol[:, l, kk * DC:(kk + 1) * DC, s], scalar=1.0,
                        in1=self.nwcol[:, l, w_, :], op0=ALU.add, op1=ALU.mult),
                        reads=[self.modcol, self.nwcol], writes=[self.SC])
        if "mod" in self.dbg:
            self.dump("modcol", self.modcol[:], [128, 2, 6 * DC, 2], self.modcol)

    def load_gate(self, l, k):
        D = self.cfg.D
        self.S.dma("sp", self.gate_b[:], self.modrow_d[l, 0, k * D:(k + 1) * D].partition_broadcast(128),
                   reads=[("modrow_d", l)], writes=[self.gate_b])
        self.S.dma("sp", self.gate_c[:], self.modrow_d[l, 1, k * D:(k + 1) * D].partition_broadcast(128),
                   reads=[("modrow_d", l)], writes=[self.gate_c])

    def norm_tile(self, ti, xt):
        cfg, S = self.cfg, self.S
        ss = self.ss
        S.op("act", lambda e: e.activation(out=self.junk[:], in_=xt[:], func=AF.Square),
             reads=[xt], writes=[self.junk])
        S.op("dve", lambda e: e.reduce_sum(out=ss[:, 0:1], in_=self.junk[:], axis=AX.X), reads=[self.junk], writes=[ss])
        S.op("dve", lambda e: e.tensor_scalar(out=ss[:, 1:2], in0=ss[:, 0:1], scalar1=1.0 / cfg.D, scalar2=cfg.EPS,
                                              op0=ALU.mult, op1=ALU.add), reads=[ss], writes=[ss])
        S.op("act", lambda e: e.activation(out=ss[:, 2:3], in_=ss[:, 1:2], func=AF.Sqrt), reads=[ss], writes=[ss])
        S.op("dve", lambda e: e.reciprocal(out=ss[:, 3:4], in_=ss[:, 2:3]), reads=[ss], writes=[ss])
        S.op("dve", lambda e: e.tensor_scalar(out=xt[:], in0=xt[:], scalar1=ss[:, 3:4], scalar2=None, op0=ALU.mult),
             reads=[xt, ss], writes=[xt])

    def norm_phase(self, l, which):
        cfg, S = self.cfg, self.S
        DC = cfg.DC
        kshift = 0 if which == 0 else 3
        for ti in range(cfg.NT):
            seq = 1 if ti < cfg.NTC else 0
            xt = self.xt[ti % 2]
            S.dma("sp", xt[:], self.xr[ti * 128:(ti + 1) * 128, :], reads=["xr"], writes=[xt])
            self.norm_tile(ti, xt)
            for c in range(DC):
                bank = self.banks[4 + (c // 4) % 2]
                S.op("pe", lambda e, c=c, bank=bank, xt=xt: e.transpose(
                    bank[:, (c % 4) * 128:(c % 4 + 1) * 128], xt[:, c * 128:(c + 1) * 128], self.ident[:]),
                    reads=[xt, self.ident], writes=[bank])
                if c % 4 == 3 or c == DC - 1:
                    for cc in range(c - (c % 4), c + 1):
                        eng = "act" if cc % 2 == 0 else "dve"
                        src = bank[:, (cc % 4) * 128:(cc % 4 + 1) * 128]
                        dst = self.hT[:, cc, ti * 128:(ti + 1) * 128]
                        sc = self.SC[:, l, which, seq, cc:cc + 1]
                        sh = self.modcol[:, l, kshift * DC + cc, seq:seq + 1]
                        if eng == "act":
                            S.op("act", lambda e, src=src, dst=dst, sc=sc, sh=sh: e.activation(
                                out=dst, in_=src, func=AF.Identity, bias=sh, scale=sc),
                                reads=[bank, self.SC, self.modcol], writes=[("hT", ti)])
                        else:
                            S.op("dve", lambda e, src=src, dst=dst, sc=sc, sh=sh: e.tensor_scalar(
                                out=dst, in0=src, scalar1=sc, scalar2=sh, op0=ALU.mult, op1=ALU.add),
                                reads=[bank, self.SC, self.modcol], writes=[("hT", ti)])
        if ("hT%d%d" % (l, which)) in self.dbg:
            self.dump("hT%d%d" % (l, which), self.hT[:], [128, DC, cfg.T], ("hT", 0), BF16)

    def hT_keys(self, t0, n):
        return [("hT", ti) for ti in range(t0 // 128, (t0 + n + 127) // 128)]

    def mlp_phase(self, l, last):
        cfg, S, A = self.cfg, self.S, self.A
        D, DC, HID = cfg.D, cfg.DC, cfg.HID
        FC = HID // 128
        if True:
            self.w1b = self.sb("w1b", [128, DC, HID], BF16)
            self.wst = [self.sb("wst%d" % i, [128, 4, 512]) for i in range(2)]
            self.w2b = [self.sb("w2b%d" % i, [128, 4, 512], BF16) for i in range(2)]
            self.hid = self.sb("hid", [128, FC, 512], BF16)
            self.xo = [self.sb("xo%d" % i, [128, 512]) for i in range(2)]
            self.rl = self.sb("rl", [128, 2, 512])
            self.wk = 0
        self.norm_phase(l, 1)
        self.load_gate(l, 5)
        for g in range(HID // 512):
            for c0 in range(0, DC, 4):
                nck = min(4, DC - c0)
                st = self.wst[self.wk % 2]
                self.wk += 1
                S.dma("sp" if self.wk % 2 else "pool", st[:, 0:nck, :],
                      A["mlp_w1"][l][c0 * 128:(c0 + nck) * 128, g * 512:(g + 1) * 512].rearrange("(c p) n -> p c n", p=128),
                      writes=[st])
                eng = "dve" if self.wk % 2 else "pool"
                S.op(eng, lambda e, st=st, g=g, c0=c0, nck=nck: e.tensor_copy(
                    out=self.w1b[:, c0:c0 + nck, g * 512:(g + 1) * 512], in_=st[:, 0:nck, :]),
                    reads=[st], writes=[("w1b", g)])
        bk = 0
        for (t0, n) in tblocks(cfg):
            if last and t0 < cfg.TC:
                continue
            isctx = t0 < cfg.TC
            for f in range(FC):
                bank = self.banks[bk % 4]
                bk += 1
                for c in range(DC):
                    S.op("pe", lambda e, f=f, c=c, bank=bank, t0=t0, n=n: e.matmul(
                        bank[:, 0:n], lhsT=self.w1b[:, c, f * 128:(f + 1) * 128], rhs=self.hT[:, c, t0:t0 + n],
                        start=(c == 0), stop=(c == DC - 1)),
                        reads=[("w1b", f // 4)] + self.hT_keys(t0, n), writes=[bank])
                S.op("act", lambda e, f=f, bank=bank, n=n: e.activation(
                    out=self.rl[:, f % 2, 0:n], in_=bank[:, 0:n], func=AF.Relu),
                    reads=[bank], writes=[("rl", f % 2)])
                S.op("dve" if f % 2 else "pool", lambda e, f=f, n=n: e.tensor_tensor(
                    out=self.hid[:, f, 0:n], in0=self.rl[:, f % 2, 0:n],
                    in1=self.rl[:, f % 2, 0:n], op=ALU.mult),
                    reads=[("rl", f % 2)], writes=[("hid", f)])
            ntile = n // 128
            for db in range(D // 512 if D >= 512 else 1):
                dn = min(512, D)
                banks = [self.banks[4 + i] for i in range(ntile)]
                for f0 in range(0, FC, 4):
                    st = self.wst[self.wk % 2]
                    wb = self.w2b[self.wk % 2]
                    self.wk += 1
                    S.dma("sp" if self.wk % 2 else "pool", st[:, :, 0:dn],
                          A["mlp_w2"][l][f0 * 128:(f0 + 4) * 128, db * 512:db * 512 + dn].rearrange("(c p) n -> p c n", p=128),
                          writes=[st])
                    S.op("dve" if self.wk % 2 else "pool", lambda e, st=st, wb=wb, dn=dn: e.tensor_copy(
                        out=wb[:, :, 0:dn], in_=st[:, :, 0:dn]), reads=[st], writes=[wb])
                    for i in range(ntile):
                        for ff in range(4):
                            f = f0 + ff
                            S.op("pe", lambda e, i=i, f=f, ff=ff, wb=wb, dn=dn, bank=banks[i]: e.matmul(
                                bank[:, 0:dn], lhsT=self.hid[:, f, i * 128:(i + 1) * 128], rhs=wb[:, ff, 0:dn],
                                start=(f == 0), stop=(f == FC - 1)),
                                reads=[("hid", f), wb], writes=[banks[i]])
                gate = self.gate_c if isctx else self.gate_b
                for i in range(ntile):
                    xo = self.xo[i % 2]
                    r0 = t0 + i * 128
                    S.dma("sp", xo[:, 0:dn], self.xr[r0:r0 + 128, db * 512:db * 512 + dn], reads=["xr"], writes=[xo])
                    S.op("dve", lambda e, i=i, xo=xo, gate=gate, db=db, dn=dn, bank=banks[i]: e.tensor_tensor(
                        out=bank[:, 0:dn], in0=bank[:, 0:dn], in1=gate[:, db * 512:db * 512 + dn], op=ALU.mult),
                        reads=[banks[i], gate], writes=[banks[i]])
                    S.op("dve", lambda e, i=i, xo=xo, dn=dn, bank=banks[i]: e.tensor_tensor(
                        out=xo[:, 0:dn], in0=xo[:, 0:dn], in1=bank[:, 0:dn], op=ALU.add),
                        reads=[banks[i], xo], writes=[xo])
                    S.dma("sp", self.xr[r0:r0 + 128, db * 512:db * 512 + dn], xo[:, 0:dn], reads=[xo], writes=["xr"])

    def final_phase(self):
        cfg, S = self.cfg, self.S
        for ti in range(cfg.NTC, cfg.NT):
            xt = self.xt[ti % 2]
            S.dma("sp", xt[:], self.xr[ti * 128:(ti + 1) * 128, :], reads=["xr"], writes=[xt])
            self.norm_tile(ti, xt)
            S.op("pool", lambda e, xt=xt: e.tensor_tensor(out=xt[:], in0=xt[:], in1=self.fncol_b[:], op=ALU.mult),
                 reads=[xt, self.fncol_b], writes=[xt])
            r0 = (ti - cfg.NTC) * 128
            o = S.dma("sp", self.out[r0:r0 + 128, :], xt[:], reads=[xt], writes=["out"])
            self.outs.append(o)


    def pcs(self, t0, n):
        off = 1 if t0 < self.cfg.TC else 3
        return slice(t0 + off, t0 + off + n)

    def col_load(self, tile_ap, vec_ap, key):
        self.S.dma("pool", tile_ap, vec_ap.rearrange("(c p) -> p c", p=128), writes=[key],
                   allow_slow_non_contiguous=True)

    def proj_chunk(self, W, col0, ncol, evac, src=None):
        cfg, S = self.cfg, self.S
        DC = cfg.DC
        src = self.hT if src is None else src
        k = self.pk
        self.pk += 1
        st, wb = self.pst[k % 2], self.pwb[k % 2]
        S.dma("sp" if k % 2 else "pool", st[:, :, 0:ncol],
              W[:, col0:col0 + ncol].rearrange("(c p) n -> p c n", p=128), writes=[st])
        S.op("pool" if k % 2 else "dve", lambda e: e.tensor_copy(out=wb[:, :, 0:ncol], in_=st[:, :, 0:ncol]),
             reads=[st], writes=[wb])
        for (t0, n) in tblocks(cfg):
            bank = self.banks[self.bk % 4]
            self.bk += 1
            for c in range(DC):
                S.op("pe", lambda e, c=c, bank=bank, t0=t0, n=n: e.matmul(
                    bank[0:ncol, 0:n], lhsT=wb[:, c, 0:ncol], rhs=src[:, c, t0:t0 + n],
                    start=(c == 0), stop=(c == DC - 1)),
                    reads=[wb] + self.hT_keys(t0, n), writes=[bank])
            evac(t0, n, bank)

    def shift_chunk(self, W, m, P, PM):
        cfg, S = self.cfg, self.S
        T = cfg.T

        def evac(t0, n, bank):
            S.op("act", lambda e: e.activation(out=P[:, self.pcs(t0, n)], in_=bank[:, 0:n], func=AF.Copy),
                 reads=[bank], writes=[P])
        self.proj_chunk(W, m * 128, 128, evac)
        tmp = self.Ptmp
        S.op("pool", lambda e: e.tensor_tensor(out=tmp[:, 1:T + 3], in0=P[:, 0:T + 2], in1=P[:, 2:T + 4], op=ALU.add),
             reads=[P], writes=[tmp])
        S.op("dve", lambda e: e.tensor_scalar(out=tmp[:, 1:T + 3], in0=tmp[:, 1:T + 3], scalar1=self.hm[:, m:m + 1],
                                              scalar2=None, op0=ALU.mult), reads=[tmp, self.hm], writes=[tmp])
        S.op("dve", lambda e: e.scalar_tensor_tensor(out=PM[:, 1:T + 3], in0=P[:, 1:T + 3], scalar=self.om[:, m:m + 1],
                                                     in1=tmp[:, 1:T + 3], op0=ALU.mult, op1=ALU.add),
             reads=[P, tmp, self.om], writes=[PM])

    def to_tm(self, srcF, key, dst, col0, padded=True, split=None):
        cfg, S = self.cfg, self.S
        ti = 0
        while ti < cfg.NT:
            hi = cfg.NTC if ti < cfg.NTC else cfg.NT
            nt = min(4, hi - ti)
            bank = self.banks[4 + self.tk % 2]
            ev = self.tev[self.tk % 2]
            self.tk += 1
            for j in range(nt):
                t0 = (ti + j) * 128
                sl = self.pcs(t0, 128) if padded else slice(t0, t0 + 128)
                S.op("pe", lambda e, j=j, sl=sl, bank=bank: e.transpose(bank[:, j * 128:(j + 1) * 128], srcF[:, sl], self.ident[:]),
                     reads=[key, self.ident], writes=[bank])
            S.op("act", lambda e, nt=nt, bank=bank, ev=ev: e.activation(out=ev[:, 0:nt * 128], in_=bank[:, 0:nt * 128], func=AF.Copy),
                 reads=[bank], writes=[ev])
            if split is None:
                S.dma("sp", dst[ti * 128:(ti + nt) * 128, col0:col0 + 128].rearrange("(j p) c -> p j c", p=128),
                      ev[:, 0:nt * 128].rearrange("p (j c) -> p j c", c=128), reads=[ev], writes=[dst.tensor.name])
            else:
                hp_ = col0 // 128
                for h2 in range(2):
                    cc = h2 * split + hp_ * 64
                    S.dma("sp" if h2 else "pool", dst[ti * 128:(ti + nt) * 128, cc:cc + 64].rearrange("(j p) c -> p j c", p=128),
                          ev[:, 0:nt * 128].rearrange("p (j c) -> p j c", c=128)[:, :, h2 * 64:(h2 + 1) * 64],
                          reads=[ev], writes=[dst.tensor.name])
            ti += nt

    def headsum(self, dst, src, n, blk, scale_eng="act"):
        S = self.S
        for o in range(0, n, 512):
            nn = min(512, n - o)
            bank = self.banks[6 + self.hk % 2]
            self.hk += 1
            S.op("pe", lambda e, o=o, nn=nn, bank=bank: e.matmul(bank[:, 0:nn], lhsT=blk[:], rhs=src[:, o:o + nn],
                                                                start=True, stop=True),
                 reads=[blk, src], writes=[bank])
            S.op("act", lambda e, o=o, nn=nn, bank=bank: e.activation(out=dst[:, o:o + nn], in_=bank[:, 0:nn], func=AF.Copy),
                 reads=[bank], writes=[dst])

    def rsqrt_inplace(self, t_ap, key, eps):
        S = self.S
        S.op("dve", lambda e: e.tensor_scalar(out=t_ap, in0=t_ap, scalar1=eps, scalar2=None, op0=ALU.add), reads=[key], writes=[key])
        S.op("act", lambda e: e.activation(out=t_ap, in_=t_ap, func=AF.Sqrt), reads=[key], writes=[key])
        S.op("dve", lambda e: e.reciprocal(out=t_ap, in_=t_ap), reads=[key], writes=[key])

    def rwkv_prep(self):
        cfg, S, A = self.cfg, self.S, self.A
        T, TC, DC, RW = cfg.T, cfg.TC, cfg.DC, cfg.RW
        NHP = RW // 128
        W = A["ab_w_in"]
        TP = T + 4
        f = self.sb
        self.pst = [f("pst%d" % i, [128, DC, 128]) for i in range(2)]
        self.pwb = [f("pwb%d" % i, [128, DC, 128], BF16) for i in range(2)]
        self.tev = [f("tev%d" % i, [128, 512]) for i in range(2)]
        P = [f("P%d" % i, [128, TP]) for i in range(2)]
        self.Ptmp = f("Ptmp", [128, TP])
        PM = [f("PM%d" % i, [128, TP]) for i in range(2)]
        F1 = [f("F1_%d" % i, [128, TP]) for i in range(4)]
        nrc = cfg.RWC // 128
        mu = f("mucol", [128, nrc])
        self.om = f("om", [128, nrc])
        self.hm = f("hm", [128, nrc])
        prm = f("rwprm", [128, 8, NHP])
        prm2 = self.rwprm2
        self.rwprm = prm
        self.rw_P, self.rw_PM = P, PM
        upst = f("upst", [128, 2, RW])
        self.upb = f("upb", [128, 2, RW], BF16)
        gst = f("gst", [128, RW])
        self.wlal = f("wlal", [128, T], BF16)
        for p in P + PM + F1:
            S.op("pool", lambda e, p=p: e.memset(p[:], 0.0), writes=[p])
        S.op("pool", lambda e: e.memset(self.Ptmp[:], 0.0), writes=[self.Ptmp])
        S.dma("sp", self.blk64[:], A["blk64"], writes=[self.blk64])
        self.col_load(mu[:], A["rw_mu"], mu)
        S.op("dve", lambda e: e.tensor_scalar(out=self.om[:], in0=mu[:], scalar1=-1.0, scalar2=1.0, op0=ALU.mult, op1=ALU.add),
             reads=[mu], writes=[self.om])
        S.op("dve", lambda e: e.tensor_scalar(out=self.hm[:], in0=mu[:], scalar1=0.5, scalar2=None, op0=ALU.mult),
             reads=[mu], writes=[self.hm])
        for i, (nm, d) in enumerate((("rw_w0", 0), ("rw_w0", 1), ("rw_a0", 0), ("rw_a0", 1))):
            self.col_load(prm[:, i, :], A[nm][d], prm)
        self.col_load(prm[:, 4, :], A["rw_k_k"], prm)
        self.col_load(prm[:, 5, :], A["rw_k_a"], prm)
        self.col_load(prm[:, 6, :], A["rw_r_k"], prm)
        self.col_load(prm2[:, 0, :], A["rw_ln_w"], prm2)
        self.col_load(prm2[:, 1, :], A["rw_ln_b"], prm2)
        S.op("dve", lambda e: e.tensor_scalar(out=prm[:, 7, :], in0=prm[:, 5, :], scalar1=-1.0, scalar2=1.0,
                                              op0=ALU.mult, op1=ALU.add), reads=[prm], writes=[prm])
        for d in range(2):
            S.dma("sp", upst[0:64, d, :], A["rw_w_up"][d], writes=[upst])
            S.dma("sp", upst[64:128, d, :], A["rw_a_up"][d], writes=[upst])
        S.op("dve", lambda e: e.tensor_copy(out=self.upb[:], in_=upst[:]), reads=[upst], writes=[self.upb])
        S.dma("sp", gst[:], A["rw_g_up"], writes=[gst])
        S.op("dve", lambda e: e.tensor_copy(out=self.gupb[:], in_=gst[:]), reads=[gst], writes=[self.gupb])
        dsc = self.dscratch
        self.kk_tm = dsc("kk_tm", [T, RW])
        self.r_tm = dsc("r_tm", [T, RW])
        self.w_tm = [dsc("w_tm%d" % d, [T, RW]) for d in range(2)]
        self.b_tm = [dsc("b_tm%d" % d, [T, RW]) for d in range(2)]
        self.kd_tm = [dsc("kd_tm%d" % d, [T, RW]) for d in range(2)]
        self.ksumF = dsc("ksumF", [RW, T])
        self.bonF = dsc("bonF", [RW, T])
        self.vF = dsc("vF", [RW, T])

        m = 3 * NHP
        self.shift_chunk(W, m, P[0], PM[0])
        S.op("act", lambda e: e.activation(out=self.wlal[0:64, 0:TC], in_=PM[0][0:64, self.pcs(0, TC)], func=AF.Tanh),
             reads=[PM[0]], writes=[self.wlal])
        S.op("act", lambda e: e.activation(out=self.wlal[0:64, TC:T], in_=PM[0][0:64, self.pcs(TC, T - TC)], func=AF.Tanh),
             reads=[PM[0]], writes=[self.wlal])
        S.op("dve", lambda e: e.tensor_copy(out=self.wlal[64:128, 0:TC], in_=PM[0][64:128, self.pcs(0, TC)]),
             reads=[PM[0]], writes=[self.wlal])
        S.op("dve", lambda e: e.tensor_copy(out=self.wlal[64:128, TC:T], in_=PM[0][64:128, self.pcs(TC, T - TC)]),
             reads=[PM[0]], writes=[self.wlal])
        self.shift_chunk(W, m + 1, P[1], PM[1])
        S.op("act", lambda e: e.activation(out=self.glt[:, 0:TC], in_=PM[1][:, self.pcs(0, TC)], func=AF.Sigmoid),
             reads=[PM[1]], writes=[self.glt])
        S.op("act", lambda e: e.activation(out=self.glt[:, TC:T], in_=PM[1][:, self.pcs(TC, T - TC)], func=AF.Sigmoid),
             reads=[PM[1]], writes=[self.glt])
        cexp = -math.exp(-0.5)
        for hp in range(NHP):
            Pk, PMk = P[hp % 2], PM[hp % 2]
            kk, sq, ad, t2 = F1
            self.shift_chunk(W, NHP + hp, Pk, PMk)
            S.op("dve", lambda e, hp=hp: e.tensor_scalar(out=kk[:], in0=PMk[:], scalar1=prm[:, 4, hp:hp + 1], scalar2=None,
                                                        op0=ALU.mult), reads=[PMk, prm], writes=[kk])
            S.op("pool", lambda e: e.tensor_tensor(out=sq[:], in0=kk[:], in1=kk[:], op=ALU.mult), reads=[kk], writes=[sq])
            self.headsum(t2, sq, TP, self.blk64)
            self.rsqrt_inplace(t2[:], t2, cfg.EPS)
            S.op("dve", lambda e: e.tensor_tensor(out=kk[:], in0=kk[:], in1=t2[:], op=ALU.mult), reads=[kk, t2], writes=[kk])
            self.to_tm(kk, kk, self.kk_tm, hp * 128, split=NHP * 64)
            for d in range(2):
                for (t0, n) in tblocks(cfg):
                    bank = self.banks[self.bk % 4]
                    self.bk += 1
                    S.op("pe", lambda e, d=d, hp=hp, t0=t0, n=n, bank=bank: e.matmul(
                        bank[:, 0:n], lhsT=self.upb[64:128, d, hp * 128:(hp + 1) * 128], rhs=self.wlal[64:128, t0:t0 + n],
                        start=True, stop=True), reads=[self.upb, self.wlal], writes=[bank])
                    S.op("act", lambda e, d=d, hp=hp, t0=t0, n=n, bank=bank: e.activation(
                        out=ad[:, self.pcs(t0, n)], in_=bank[:, 0:n], func=AF.Sigmoid, bias=prm[:, 2 + d, hp:hp + 1]),
                        reads=[bank, prm], writes=[ad])
                S.op("pool", lambda e: e.tensor_tensor(out=sq[:], in0=ad[:], in1=kk[:], op=ALU.mult), reads=[ad, kk], writes=[sq])
                self.to_tm(sq, sq, self.b_tm[d], hp * 128, split=NHP * 64)
                S.op("dve", lambda e, hp=hp: e.tensor_scalar(out=ad[:], in0=ad[:], scalar1=prm[:, 5, hp:hp + 1],
                                                            scalar2=prm[:, 7, hp:hp + 1], op0=ALU.mult, op1=ALU.add),
                     reads=[ad, prm], writes=[ad])
                S.op("dve", lambda e: e.tensor_tensor(out=ad[:], in0=ad[:], in1=PMk[:], op=ALU.mult), reads=[ad, PMk], writes=[ad])
                self.to_tm(ad, ad, self.kd_tm[d], hp * 128, split=NHP * 64)
                if d == 0:
                    S.op("pool", lambda e: e.tensor_copy(out=t2[:], in_=ad[:]), reads=[ad], writes=[t2])
                else:
                    S.op("pool", lambda e: e.tensor_tensor(out=t2[:], in0=t2[:], in1=ad[:], op=ALU.add), reads=[ad, t2], writes=[t2])
                    S.dma("sp", self.ksumF[hp * 128:(hp + 1) * 128, 0:TC], t2[:, self.pcs(0, TC)], reads=[t2], writes=["ksumF"])
                    S.dma("sp", self.ksumF[hp * 128:(hp + 1) * 128, TC:T], t2[:, self.pcs(TC, T - TC)], reads=[t2], writes=["ksumF"])
                for (t0, n) in tblocks(cfg):
                    bank = self.banks[self.bk % 4]
                    self.bk += 1
                    S.op("pe", lambda e, d=d, hp=hp, t0=t0, n=n, bank=bank: e.matmul(
                        bank[:, 0:n], lhsT=self.upb[0:64, d, hp * 128:(hp + 1) * 128], rhs=self.wlal[0:64, t0:t0 + n],
                        start=True, stop=True), reads=[self.upb, self.wlal], writes=[bank])
                    S.op("act", lambda e, d=d, hp=hp, t0=t0, n=n, bank=bank: e.activation(
                        out=sq[:, self.pcs(t0, n)], in_=bank[:, 0:n], func=AF.Sigmoid, bias=prm[:, d, hp:hp + 1]),
                        reads=[bank, prm], writes=[sq])
                S.op("act", lambda e: e.activation(out=sq[:], in_=sq[:], func=AF.Exp, scale=cexp), reads=[sq], writes=[sq])
                self.to_tm(sq, sq, self.w_tm[d], hp * 128, split=NHP * 64)
        for hp in range(NHP):
            Pk, PMk = P[hp % 2], PM[hp % 2]
            kk, sq, ad, t2 = F1
            self.shift_chunk(W, hp, Pk, PMk)
            self.to_tm(PMk, PMk, self.r_tm, hp * 128, split=NHP * 64)
            S.dma("sp", kk[:, self.pcs(0, TC)], self.ksumF[hp * 128:(hp + 1) * 128, 0:TC], reads=["ksumF"], writes=[kk])
            S.dma("sp", kk[:, self.pcs(TC, T - TC)], self.ksumF[hp * 128:(hp + 1) * 128, TC:T], reads=["ksumF"], writes=[kk])
            S.op("dve", lambda e, hp=hp: e.scalar_tensor_tensor(out=sq[:], in0=PMk[:], scalar=prm[:, 6, hp:hp + 1], in1=kk[:],
                                                               op0=ALU.mult, op1=ALU.mult), reads=[PMk, prm, kk], writes=[sq])
            self.headsum(t2, sq, TP, self.blk64)
            S.dma("sp", self.bonF[hp * 128:(hp + 1) * 128, 0:TC], t2[:, self.pcs(0, TC)], reads=[t2], writes=["bonF"])
            S.dma("sp", self.bonF[hp * 128:(hp + 1) * 128, TC:T], t2[:, self.pcs(TC, T - TC)], reads=[t2], writes=["bonF"])
        for hp in range(NHP):
            Pk, PMk = P[hp % 2], PM[hp % 2]
            self.shift_chunk(W, 2 * NHP + hp, Pk, PMk)
            S.dma("sp", self.vF[hp * 128:(hp + 1) * 128, 0:TC], PMk[:, self.pcs(0, TC)], reads=[PMk], writes=["vF"])
            S.dma("sp", self.vF[hp * 128:(hp + 1) * 128, TC:T], PMk[:, self.pcs(TC, T - TC)], reads=[PMk], writes=["vF"])

    def rwkv_scan(self):
        cfg, S, A = self.cfg, self.S, self.A
        T, TC, RW = cfg.T, cfg.TC, cfg.RW
        NHP = RW // 128
        f = self.sb
        TB = 4
        NW = NHP * 64
        St = f("rwS", [128, NHP, 64])
        tmp = f("rwtmp", [128, NHP, 64])
        sa = f("rwsa", [128, NHP])
        XB = [f("rwXB%d" % i, [2, TB, 5, NW]) for i in range(1)]
        VB = [f("rwVB%d" % i, [128, NHP, TB]) for i in range(2)]
        self.Y = [f("rwY%d" % d, [128, NHP, T]) for d in range(2)]
        selH = f("selH", [2, 128])
        S.dma("sp", selH[:], A["selH"], writes=[selH])
        nb = 0
        stp = 0
        for d in range(2):
            S.op("dve", lambda e: e.memset(St[:], 0.0), writes=[St])
            if d == 0:
                blocks = [(b0, False) for b0 in range(0, T, TB)]
            else:
                blocks = [(b0, True) for b0 in range(TC - TB, -1, -TB)] + [(b0, True) for b0 in range(T - TB, TC - 1, -TB)]
            srcs = [self.kk_tm, self.w_tm[d], self.b_tm[d], self.kd_tm[d], self.r_tm]
            for (b0, rev) in blocks:
                xb, vb = XB[0], VB[nb % 2]
                nb += 1
                for a, src in enumerate(srcs):
                    S.dma("sp" if a % 2 else "pool", xb[:, :, a, :],
                          src[b0:b0 + TB, :].rearrange("t (h n) -> h t n", h=2),
                          reads=[src.tensor.name], writes=[xb])
                S.dma("sp", vb[:], self.vF[:, b0:b0 + TB].rearrange("(hp p) t -> p hp t", p=128),
                      reads=["vF"], writes=[vb], allow_slow_non_contiguous=True)
                js = range(TB - 1, -1, -1) if rev else range(TB)
                for j in js:
                    t = b0 + j
                    bs = [self.banks[(stp % 2) * 3 + i] for i in range(3)]
                    stp += 1
                    for a in range(5):
                        bank = bs[a // 2]
                        o = (a % 2) * NW
                        S.op("pe", lambda e, a=a, j=j, bank=bank, o=o, xb=xb: e.matmul(
                            bank[:, o:o + NW], lhsT=selH[:], rhs=xb[:, j, a, :], start=True, stop=True),
                            reads=[selH, xb], writes=[bank])
                    v3 = lambda bank, o: bank[:, o:o + NW].rearrange("p (h k) -> p h k", k=64)
                    kkB, wB, bB, kB, rB = v3(bs[0], 0), v3(bs[0], NW), v3(bs[1], 0), v3(bs[1], NW), v3(bs[2], 0)
                    S.op("dve", lambda e, kkB=kkB: e.tensor_tensor(out=tmp[:], in0=St[:], in1=kkB, op=ALU.mult),
                         reads=[St, bs[0]], writes=[tmp])
                    S.op("dve", lambda e: e.tensor_reduce(out=sa[:], in_=tmp[:], axis=AX.X, op=ALU.add), reads=[tmp], writes=[sa])
                    S.op("dve", lambda e, wB=wB: e.tensor_tensor(out=St[:], in0=St[:], in1=wB, op=ALU.mult),
                         reads=[St, bs[0]], writes=[St])
                    S.op("dve", lambda e, bB=bB: e.tensor_tensor(out=tmp[:], in0=bB, in1=sa[:].unsqueeze(2).to_broadcast([128, NHP, 64]),
                                                               op=ALU.mult), reads=[sa, bs[1]], writes=[tmp])
                    S.op("dve", lambda e: e.tensor_tensor(out=St[:], in0=St[:], in1=tmp[:], op=ALU.subtract), reads=[St, tmp], writes=[St])
                    S.op("dve", lambda e, kB=kB, vb=vb, j=j: e.tensor_tensor(
                        out=tmp[:], in0=kB, in1=vb[:, :, j:j + 1].to_broadcast([128, NHP, 64]), op=ALU.mult),
                        reads=[vb, bs[1]], writes=[tmp])
                    S.op("dve", lambda e: e.tensor_tensor(out=St[:], in0=St[:], in1=tmp[:], op=ALU.add), reads=[St, tmp], writes=[St])
                    S.op("dve", lambda e, rB=rB: e.tensor_tensor(out=tmp[:], in0=St[:], in1=rB, op=ALU.mult),
                         reads=[St, bs[2]], writes=[tmp])
                    S.op("dve", lambda e, t=t, d=d: e.tensor_reduce(out=self.Y[d][:, :, t], in_=tmp[:], axis=AX.X, op=ALU.add),
                         reads=[tmp], writes=[("rwY", d)])

    def rwkv_out(self):
        cfg, S, A = self.cfg, self.S, self.A
        T, TC, RW = cfg.T, cfg.TC, cfg.RW
        NHP = RW // 128
        f = self.sb
        y = f("roy", [128, T])
        m = f("rom", [128, T])
        sq = f("rosq", [128, T])
        blkm = f("blkm", [128, 128])
        S.op("dve", lambda e: e.tensor_scalar(out=blkm[:], in0=self.blk64[:], scalar1=1.0 / 64, scalar2=None, op0=ALU.mult),
             reads=[self.blk64], writes=[blkm])
        prm2 = self.rwprm2
        for hp in range(NHP):
            S.op("dve", lambda e, hp=hp: e.tensor_tensor(out=y[:], in0=self.Y[0][:, hp, :], in1=self.Y[1][:, hp, :], op=ALU.add),
                 reads=[("rwY", 0), ("rwY", 1)], writes=[y])
            self.headsum(m, y, T, blkm)
            S.op("dve", lambda e: e.tensor_tensor(out=y[:], in0=y[:], in1=m[:], op=ALU.subtract), reads=[y, m], writes=[y])
            S.op("pool", lambda e: e.tensor_tensor(out=sq[:], in0=y[:], in1=y[:], op=ALU.mult), reads=[y], writes=[sq])
            self.headsum(m, sq, T, blkm)
            self.rsqrt_inplace(m[:], m, 64e-5)
            S.op("dve", lambda e: e.tensor_tensor(out=y[:], in0=y[:], in1=m[:], op=ALU.mult), reads=[y, m], writes=[y])
            S.op("dve", lambda e, hp=hp: e.tensor_scalar(out=y[:], in0=y[:], scalar1=prm2[:, 0, hp:hp + 1], scalar2=prm2[:, 1, hp:hp + 1],
                                                        op0=ALU.mult, op1=ALU.add), reads=[y, prm2], writes=[y])
            S.dma("sp", m[:], self.bonF[hp * 128:(hp + 1) * 128, :], reads=["bonF"], writes=[m])
            S.dma("pool", sq[:], self.vF[hp * 128:(hp + 1) * 128, :], reads=["vF"], writes=[sq])
            S.op("pool", lambda e: e.tensor_tensor(out=m[:], in0=m[:], in1=sq[:], op=ALU.mult), reads=[m, sq], writes=[m])
            S.op("dve", lambda e: e.tensor_tensor(out=y[:], in0=y[:], in1=m[:], op=ALU.add), reads=[y, m], writes=[y])
            for (t0, n) in tblocks(cfg):
                bank = self.banks[self.bk % 4]
                self.bk += 1
                S.op("pe", lambda e, hp=hp, t0=t0, n=n, bank=bank: e.matmul(
                    bank[:, 0:n], lhsT=self.gupb[:, hp * 128:(hp + 1) * 128], rhs=self.glt[:, t0:t0 + n], start=True, stop=True),
                    reads=[self.gupb, self.glt], writes=[bank])
                S.op("dve", lambda e, hp=hp, t0=t0, n=n, bank=bank: e.tensor_tensor(
                    out=self.catT[:, hp, t0:t0 + n], in0=y[:, t0:t0 + n], in1=bank[:, 0:n], op=ALU.mult),
                    reads=[y, bank], writes=[("catT", hp)])

    def hyena_prep(self, P, PM):
        cfg, S, A = self.cfg, self.S, self.A
        T, TC, HY = cfg.T, cfg.TC, cfg.HY
        NJ = HY // 128
        W = A["ab_w_in"]
        f = self.sb
        cw = f("hycw", [128, 4, 3 * NJ])
        for j in range(3):
            self.col_load(cw[:, j, :], A["hy_conv_w"][j], cw)
        self.col_load(cw[:, 3, :], A["hy_conv_b"], cw)
        self.sF = self.dscratch("hy_sF", [HY, T])
        self.x0F = self.dscratch("hy_x0F", [HY, T])
        Ca, Cb = PM

        def conv(m, Pt, out):
            def evac(t0, n, bank):
                S.op("act", lambda e: e.activation(out=Pt[:, self.pcs(t0, n)], in_=bank[:, 0:n], func=AF.Copy),
                     reads=[bank], writes=[Pt])
            self.proj_chunk(W, cfg.RWC + m * 128, 128, evac)
            S.op("dve", lambda e: e.tensor_scalar(out=out[:, 1:T + 3], in0=Pt[:, 0:T + 2], scalar1=cw[:, 0, m:m + 1],
                                                  scalar2=cw[:, 3, m:m + 1], op0=ALU.mult, op1=ALU.add),
                 reads=[Pt, cw], writes=[out])
            S.op("dve", lambda e: e.scalar_tensor_tensor(out=out[:, 1:T + 3], in0=Pt[:, 1:T + 3], scalar=cw[:, 1, m:m + 1],
                                                         in1=out[:, 1:T + 3], op0=ALU.mult, op1=ALU.add),
                 reads=[Pt, cw, out], writes=[out])
            S.op("dve", lambda e: e.scalar_tensor_tensor(out=out[:, 1:T + 3], in0=Pt[:, 2:T + 4], scalar=cw[:, 2, m:m + 1],
                                                         in1=out[:, 1:T + 3], op0=ALU.mult, op1=ALU.add),
                 reads=[Pt, cw, out], writes=[out])
        for j in range(NJ):
            conv(NJ + j, P[0], Ca)
            conv(2 * NJ + j, P[1], Cb)
            S.op("pool", lambda e: e.tensor_tensor(out=Ca[:], in0=Ca[:], in1=Cb[:], op=ALU.mult), reads=[Ca, Cb], writes=[Ca])
            for (a0, n) in ((0, TC), (TC, T - TC)):
                S.dma("sp", self.sF[j * 128:(j + 1) * 128, a0:a0 + n], Ca[:, self.pcs(a0, n)], reads=[Ca], writes=["hy_sF"])
            conv(j, P[0], Cb)
            for (a0, n) in ((0, TC), (TC, T - TC)):
                S.dma("sp", self.x0F[j * 128:(j + 1) * 128, a0:a0 + n], Cb[:, self.pcs(a0, n)], reads=[Cb], writes=["hy_x0F"])

    def hyena_main(self):
        cfg, S, A = self.cfg, self.S, self.A
        T, TC, TL, HY = cfg.T, cfg.TC, cfg.TL, cfg.HY
        NJ = HY // 128
        NHP = cfg.RW // 128
        f = self.sb
        LM = max(TC, TL)
        w1 = f("hyw1", [33, 64]); w2 = f("hyw2", [64, 64]); w3 = f("hyw3", [64, 64]); w4 = f("hyw4", [64, 2 * HY])
        fb = f("hyfb", [64, 8])
        dl = f("hydl", [128, 2 * NJ])
        skp = f("hyskip", [128, NJ])
        zT = f("hyz", [33, LM]); tr = f("hytr", [128, LM])
        hA = f("hyhA", [64, LM]); hB = f("hyhB", [64, LM]); arg = f("hyarg", [64, LM]); msk = f("hymsk", [64, LM])
        hf = f("hyhf", [128, LM]); hb = f("hyhb", [128, LM]); jk = f("hyjk", [128, LM])
        sj = f("hysj", [128, LM]); yj = f("hyyj", [128, LM]); xj = f("hyxj", [128, LM])
        nrm = f("hynrm", [128, 4])
        for t_, nm in ((w1, "hy_f_w1"), (w2, "hy_f_w2"), (w3, "hy_f_w3"), (w4, "hy_f_w4")):
            S.dma("sp", t_[:], A[nm], writes=[t_])
        for i, nm in enumerate(("hy_freq", "hy_f_b1", "hy_f_b2", "hy_f_b3")):
            S.dma("pool", fb[:, i:i + 1], A[nm].rearrange("(p o) -> p o", o=1), writes=[fb])
        for i in range(3):
            S.op("dve", lambda e, i=i: e.tensor_tensor(out=fb[:, 4 + i:5 + i], in0=fb[:, 1 + i:2 + i], in1=fb[:, 0:1], op=ALU.mult),
                 reads=[fb], writes=[fb])
        S.op("dve", lambda e: e.memset(fb[:, 7:8], -math.pi), writes=[fb])
        self.col_load(dl[:], A["hy_negdelta"], dl)
        self.col_load(skp[:], A["hy_skip"], skp)
        for (a0, L, znm, tnm) in ((0, TC, "hy_z_ctx", "hy_t_ctx"), (TC, TL, "hy_z_lat", "hy_t_lat")):
            S.dma("sp", zT[:, 0:L], A[znm], writes=[zT])
            S.dma("sp", tr[:, 0:L], A[tnm].partition_broadcast(128), writes=[tr])
            src, K = zT, 33
            for li, (w_, dst) in enumerate(((w1, hA), (w2, hB), (w3, hA))):
                for o in range(0, L, 512):
                    n = min(512, L - o)
                    bank = self.banks[self.bk % 4]
                    self.bk += 1
                    S.op("pe", lambda e, w_=w_, src=src, K=K, o=o, n=n, bank=bank: e.matmul(
                        bank[0:64, 0:n], lhsT=w_[0:K, :], rhs=src[0:K, o:o + n], start=True, stop=True),
                        reads=[w_, src], writes=[bank])
                    S.op("dve", lambda e, li=li, o=o, n=n, bank=bank: e.tensor_scalar(
                        out=arg[:, o:o + n], in0=bank[0:64, 0:n], scalar1=fb[:, 0:1], scalar2=fb[:, 4 + li:5 + li],
                        op0=ALU.mult, op1=ALU.add), reads=[bank, fb], writes=[arg])
                S.op("dve", lambda e: e.tensor_scalar(out=msk[:, 0:L], in0=arg[:, 0:L], scalar1=math.pi, scalar2=-2 * math.pi,
                                                      op0=ALU.is_gt, op1=ALU.mult), reads=[arg], writes=[msk])
                S.op("dve", lambda e: e.tensor_tensor(out=msk[:, 0:L], in0=msk[:, 0:L], in1=arg[:, 0:L], op=ALU.add),
                     reads=[arg, msk], writes=[msk])
                S.op("dve", lambda e: e.tensor_scalar(out=arg[:, 0:L], in0=arg[:, 0:L], scalar1=-math.pi, scalar2=2 * math.pi,
                                                      op0=ALU.is_lt, op1=ALU.mult), reads=[arg], writes=[arg])
                S.op("dve", lambda e: e.tensor_tensor(out=msk[:, 0:L], in0=msk[:, 0:L], in1=arg[:, 0:L], op=ALU.add),
                     reads=[arg, msk], writes=[msk])
                S.op("act", lambda e, dst=dst: e.activation(out=dst[:, 0:L], in_=msk[:, 0:L], func=AF.Sin),
                     reads=[msk], writes=[dst])
                src, K = dst, 64
            h3 = hA
            for j in range(NJ):
                for (dst, cc) in ((hf, j), (hb, NJ + j)):
                    S.op("act", lambda e, cc=cc: e.activation(out=jk[:, 0:L], in_=tr[:, 0:L], func=AF.Exp, scale=dl[:, cc:cc + 1]),
                         reads=[tr, dl], writes=[jk])
                    for o in range(0, L, 512):
                        n = min(512, L - o)
                        bank = self.banks[self.bk % 4]
                        self.bk += 1
                        S.op("pe", lambda e, cc=cc, o=o, n=n, bank=bank: e.matmul(
                            bank[:, 0:n], lhsT=w4[:, cc * 128:(cc + 1) * 128], rhs=h3[:, o:o + n], start=True, stop=True),
                            reads=[w4, h3], writes=[bank])
                        S.op("dve", lambda e, dst=dst, o=o, n=n, bank=bank: e.tensor_tensor(
                            out=dst[:, o:o + n], in0=jk[:, o:o + n], in1=bank[:, 0:n], op=ALU.mult),
                            reads=[bank, jk], writes=[dst])
                S.op("dve", lambda e: e.memset(hb[:, 0:1], 0.0), reads=[hb], writes=[hb])
                for i, src_ in enumerate((hf, hb)):
                    S.op("act", lambda e, src_=src_: e.activation(out=jk[:, 0:L], in_=src_[:, 0:L], func=AF.Abs),
                         reads=[src_], writes=[jk])
                    S.op("dve", lambda e, i=i: e.reduce_sum(out=nrm[:, i:i + 1], in_=jk[:, 0:L], axis=AX.X), reads=[jk], writes=[nrm])
                S.op("dve", lambda e: e.tensor_tensor(out=nrm[:, 2:3], in0=nrm[:, 0:1], in1=nrm[:, 1:2], op=ALU.add), reads=[nrm], writes=[nrm])
                S.op("dve", lambda e: e.reciprocal(out=nrm[:, 3:4], in_=nrm[:, 2:3]), reads=[nrm], writes=[nrm])
                S.dma("sp", sj[:, 0:L], self.sF[j * 128:(j + 1) * 128, a0:a0 + L], reads=["hy_sF"], writes=[sj])
                S.dma("sp", xj[:, 0:L], self.x0F[j * 128:(j + 1) * 128, a0:a0 + L], reads=["hy_x0F"], writes=[xj])
                S.op("pool", lambda e: e.tensor_scalar(out=yj[:, 0:L], in0=sj[:, 0:L], scalar1=hf[:, 0:1], scalar2=None, op0=ALU.mult),
                     reads=[sj, hf], writes=[yj])
                for dlt in range(1, L):
                    S.op("dve", lambda e, dlt=dlt: e.scalar_tensor_tensor(
                        out=yj[:, dlt:L], in0=sj[:, 0:L - dlt], scalar=hf[:, dlt:dlt + 1], in1=yj[:, dlt:L], op0=ALU.mult, op1=ALU.add),
                        reads=[sj, hf, yj], writes=[yj])
                    S.op("dve", lambda e, dlt=dlt: e.scalar_tensor_tensor(
                        out=yj[:, 0:L - dlt], in0=sj[:, dlt:L], scalar=hb[:, dlt:dlt + 1], in1=yj[:, 0:L - dlt], op0=ALU.mult, op1=ALU.add),
                        reads=[sj, hb, yj], writes=[yj])
                S.op("pool", lambda e: e.tensor_scalar(out=yj[:, 0:L], in0=yj[:, 0:L], scalar1=nrm[:, 3:4], scalar2=None, op0=ALU.mult),
                     reads=[yj, nrm], writes=[yj])
                S.op("dve", lambda e, j=j: e.scalar_tensor_tensor(out=yj[:, 0:L], in0=sj[:, 0:L], scalar=skp[:, j:j + 1], in1=yj[:, 0:L],
                                                                  op0=ALU.mult, op1=ALU.add), reads=[sj, skp, yj], writes=[yj])
                S.op("pool", lambda e, j=j, a0=a0, L=L: e.tensor_tensor(out=self.catT[:, NHP + j, a0:a0 + L], in0=yj[:, 0:L], in1=xj[:, 0:L],
                                                                       op=ALU.mult), reads=[yj, xj], writes=[("catT", NHP + j)])

    def out_proj(self, inT, KC, Wd, l, gk, skip_ctx, tag):
        cfg, S = self.cfg, self.S
        D = cfg.D
        f = self.sb
        wob = f("wob" + tag, [128, KC, D], BF16)
        wst = [f("wost%s%d" % (tag, i), [128, D]) for i in range(2)]
        xo = [f("wxo%s%d" % (tag, i), [128, 512]) for i in range(2)]
        self.load_gate(l, gk)
        for c in range(KC):
            st = wst[c % 2]
            S.dma("sp" if c % 2 else "pool", st[:], Wd[c * 128:(c + 1) * 128, :], writes=[st])
            S.op("dve" if c % 2 else "pool", lambda e, c=c, st=st: e.tensor_copy(out=wob[:, c, :], in_=st[:]),
                 reads=[st], writes=[(wob, c)])
        k = 0
        for ti in range(cfg.NT):
            if skip_ctx and ti < cfg.NTC:
                continue
            gate = self.gate_c if ti < cfg.NTC else self.gate_b
            for db in range(max(1, D // 512)):
                dn = min(512, D)
                bank = self.banks[4 + k % 4]
                x_ = xo[k % 2]
                k += 1
                for c in range(KC):
                    S.op("pe", lambda e, c=c, ti=ti, db=db, dn=dn, bank=bank: e.matmul(
                        bank[:, 0:dn], lhsT=inT[:, c, ti * 128:(ti + 1) * 128], rhs=wob[:, c, db * 512:db * 512 + dn],
                        start=(c == 0), stop=(c == KC - 1)), reads=[(inT, c), (wob, c)], writes=[bank])
                r0 = ti * 128
                S.dma("sp", x_[:, 0:dn], self.xr[r0:r0 + 128, db * 512:db * 512 + dn], reads=["xr"], writes=[x_])
                S.op("dve", lambda e, bank=bank, gate=gate, db=db, dn=dn: e.tensor_tensor(
                    out=bank[:, 0:dn], in0=bank[:, 0:dn], in1=gate[:, db * 512:db * 512 + dn], op=ALU.mult),
                    reads=[bank, gate], writes=[bank])
                S.op("dve", lambda e, bank=bank, x_=x_, dn=dn: e.tensor_tensor(
                    out=x_[:, 0:dn], in0=x_[:, 0:dn], in1=bank[:, 0:dn], op=ALU.add), reads=[bank, x_], writes=[x_])
                S.dma("sp", self.xr[r0:r0 + 128, db * 512:db * 512 + dn], x_[:, 0:dn], reads=[x_], writes=["xr"])

    def layer0(self):
        cfg, S = self.cfg, self.S
        T = cfg.T
        TP = T + 4
        NHP = cfg.RW // 128
        self.scope_begin()
        self.rwprm2 = self.sb("rwprm2", [128, 2, NHP])
        self.blk64 = self.sb("blk64", [128, 128])
        self.gupb = self.sb("gupb", [128, cfg.RW], BF16)
        self.glt = self.sb("glt", [128, T], BF16)
        self.scope_begin()
        self.hT = self.sb("hT_a", [128, cfg.DC, T], BF16)
        self.norm_phase(0, 0)
        self.rwkv_prep()
        self.hyena_prep(self.rw_P, self.rw_PM)
        self.scope_end()
        self.catT = self.sb("catT", [128, cfg.DC, T], BF16)
        self.scope_begin()
        self.hyena_main()
        self.scope_end()
        self.scope_begin()
        self.rwkv_scan()
        self.rwkv_out()
        self.scope_end()
        self.scope_begin()
        self.out_proj(self.catT, cfg.DC, self.A["ab_w_out"], 0, 2, False, "a")
        self.scope_end()
        self.scope_end()


    def layer1(self):
        cfg, S, A = self.cfg, self.S, self.A
        T, TC, D, DC, H = cfg.T, cfg.TC, cfg.D, cfg.DC, cfg.DNH
        TP = T + 4
        f = self.sb
        W = A["dn_w_in"]
        dsc = self.dscratch
        kq_tm = dsc("dn_kq_tm", [T, 2 * D])
        gb_tm = [dsc("dn_gb_tm%d" % d, [T, 2 * H]) for d in range(2)]
        vF = dsc("dn_vF", [D, T])
        zF = dsc("dn_zF", [D, T])
        self.scope_begin()
        self.catT = f("catT1", [128, DC, T], BF16)
        ones128 = f("ones128", [128, 128])
        S.op("dve", lambda e: e.memset(ones128[:], 1.0), writes=[ones128])
        nwc = f("dnnw", [128, 1])
        S.dma("pool", nwc[:], A["dn_norm"].rearrange("(p o) -> p o", o=1), writes=[nwc])
        self.scope_begin()
        self.hT = f("hT_d", [128, DC, T], BF16)
        self.norm_phase(1, 0)
        self.pst = [f("dpst%d" % i, [128, DC, 128]) for i in range(2)]
        self.pwb = [f("dpwb%d" % i, [128, DC, 128], BF16) for i in range(2)]
        self.tev = [f("dtev%d" % i, [128, 512]) for i in range(2)]
        P = [f("dP%d" % i, [128, TP]) for i in range(2)]
        C = [f("dC%d" % i, [128, TP]) for i in range(3)]
        for p in P + C:
            S.op("pool", lambda e, p=p: e.memset(p[:], 0.0), writes=[p])
        cw = f("dncw", [128, 3, 3 * DC])
        for j in range(3):
            self.col_load(cw[:, j, :], A["dn_conv_w"][j], cw)
        for m in range(3 * DC):
            Pt, out = P[m % 2], C[0]

            def evac(t0, n, bank, Pt=Pt):
                S.op("act", lambda e: e.activation(out=Pt[:, self.pcs(t0, n)], in_=bank[:, 0:n], func=AF.Copy),
                     reads=[bank], writes=[Pt])
            self.proj_chunk(W, m * 128, 128, evac)
            S.op("dve", lambda e: e.tensor_scalar(out=out[:, 1:T + 3], in0=Pt[:, 0:T + 2], scalar1=cw[:, 0, m:m + 1], scalar2=None,
                                                  op0=ALU.mult), reads=[Pt, cw], writes=[out])
            S.op("dve", lambda e: e.scalar_tensor_tensor(out=out[:, 1:T + 3], in0=Pt[:, 1:T + 3], scalar=cw[:, 1, m:m + 1],
                                                         in1=out[:, 1:T + 3], op0=ALU.mult, op1=ALU.add), reads=[Pt, cw, out], writes=[out])
            S.op("dve", lambda e: e.scalar_tensor_tensor(out=out[:, 1:T + 3], in0=Pt[:, 2:T + 4], scalar=cw[:, 2, m:m + 1],
                                                         in1=out[:, 1:T + 3], op0=ALU.mult, op1=ALU.add), reads=[Pt, cw, out], writes=[out])
            S.op("act", lambda e: e.activation(out=out[:], in_=out[:], func=AF.Silu), reads=[out], writes=[out])
            if m < 2 * DC:
                sq, rn = C[1], C[2]
                S.op("pool", lambda e: e.tensor_tensor(out=sq[:], in0=out[:], in1=out[:], op=ALU.mult), reads=[out], writes=[sq])
                self.headsum(rn, sq, TP, ones128)
                self.rsqrt_inplace(rn[:], rn, cfg.EPS)
                sc = (128.0 ** -0.5) if m < DC else 1.0
                S.op("dve", lambda e: e.scalar_tensor_tensor(out=out[:], in0=out[:], scalar=sc, in1=rn[:], op0=ALU.mult, op1=ALU.mult),
                     reads=[out, rn], writes=[out])
                col = (D + m * 128) if m < DC else ((m - DC) * 128)
                self.to_tm(out, out, kq_tm, col)
            else:
                h = m - 2 * DC
                for (a0, n) in ((0, TC), (TC, T - TC)):
                    S.dma("sp", vF[h * 128:(h + 1) * 128, a0:a0 + n], out[:, self.pcs(a0, n)], reads=[out], writes=["dn_vF"])
        for m in range(3 * DC, 4 * DC):
            out = C[m % 2]

            def evacz(t0, n, bank, out=out):
                S.op("act", lambda e: e.activation(out=out[:, t0:t0 + n], in_=bank[:, 0:n], func=AF.Silu), reads=[bank], writes=[out])
            self.proj_chunk(W, m * 128, 128, evacz)
            h = m - 3 * DC
            S.dma("sp", zF[h * 128:(h + 1) * 128, :], out[:, 0:T], reads=[out], writes=["dn_zF"])
        NG = 4 * H
        gp = f("dngp", [128, 4])
        S.op("dve", lambda e: e.memset(gp[:], 0.0), writes=[gp])
        S.dma("pool", gp[0:2 * H, 0:1], A["dn_A_log"].rearrange("d (h o) -> (d h) o", o=1), reads=[gp], writes=[gp])
        S.dma("pool", gp[0:2 * H, 1:2], A["dn_dt_bias"].rearrange("d (h o) -> (d h) o", o=1), reads=[gp], writes=[gp])
        S.op("act", lambda e: e.activation(out=gp[:, 2:3], in_=gp[:, 0:1], func=AF.Exp), reads=[gp], writes=[gp])
        S.op("dve", lambda e: e.tensor_scalar(out=gp[:, 3:4], in0=gp[:, 2:3], scalar1=-1.0, scalar2=None, op0=ALU.mult), reads=[gp], writes=[gp])
        E, Bt = C[0], C[1]

        def evacg(t0, n, bank):
            S.op("act", lambda e: e.activation(out=E[0:NG, t0:t0 + n], in_=bank[0:NG, 0:n], func=AF.Exp, bias=gp[0:NG, 1:2]),
                 reads=[bank, gp], writes=[E])
            S.op("act", lambda e: e.activation(out=Bt[0:NG, t0:t0 + n], in_=bank[0:NG, 0:n], func=AF.Sigmoid), reads=[bank], writes=[Bt])
        self.proj_chunk(W, 4 * D, NG, evacg)
        S.op("act", lambda e: e.activation(out=E[0:NG, 0:T], in_=E[0:NG, 0:T], func=AF.Ln, bias=1.0), reads=[E], writes=[E])
        S.op("act", lambda e: e.activation(out=E[0:NG, 0:T], in_=E[0:NG, 0:T], func=AF.Exp, scale=gp[0:NG, 3:4]), reads=[E, gp], writes=[E])
        ev2 = f("dnev2", [128, 2, NG])
        for ti in range(cfg.NT):
            bank = self.banks[4 + ti % 2]
            for i, src in enumerate((E, Bt)):
                S.op("pe", lambda e, i=i, src=src, ti=ti, bank=bank: e.transpose(
                    bank[:, i * NG:(i + 1) * NG], src[0:NG, ti * 128:(ti + 1) * 128], self.ident[0:NG, 0:NG]),
                    reads=[src, self.ident], writes=[bank])
            S.op("act", lambda e, bank=bank: e.activation(out=ev2[:], in_=bank[:, 0:2 * NG].rearrange("p (i g) -> p i g", i=2), func=AF.Copy),
                 reads=[bank], writes=[ev2])
            for d in range(2):
                S.dma("sp", gb_tm[d][ti * 128:(ti + 1) * 128, 0:H], ev2[:, 0, d * H:(d + 1) * H], reads=[ev2], writes=[gb_tm[d].tensor.name])
                S.dma("sp", gb_tm[d][ti * 128:(ti + 1) * 128, H:2 * H], ev2[:, 1, 2 * H + d * H:2 * H + (d + 1) * H], reads=[ev2],
                      writes=[gb_tm[d].tensor.name])
        self.scope_end()
        self.scope_begin()
        TB = 2
        HW = H * 128
        St = f("dnS", [128, H, 128]); tmp = f("dntmp", [128, H, 128])
        sk = f("dnsk", [128, H]); dd = f("dndd", [128, H])
        XB = f("dnXB", [1, TB, 2 * HW]); GB = f("dnGB", [1, TB, 2 * H])
        VB = [f("dnVB%d" % i, [128, H, TB]) for i in range(2)]
        O1 = f("dnO", [128, H, T])
        O = [O1, O1]
        ot = f("dnot", [128, H])
        ones1 = f("ones1", [1, 128])
        S.op("dve", lambda e: e.memset(ones1[:], 1.0), writes=[ones1])
        nq = (2 * HW + 511) // 512
        nbk = 0
        stp = 0
        for d in range(2):
            S.op("dve", lambda e: e.memset(St[:], 0.0), writes=[St])
            if d == 0:
                blocks = [(b0, False) for b0 in range(0, T, TB)]
            else:
                blocks = [(b0, True) for b0 in range(TC - TB, -1, -TB)] + [(b0, True) for b0 in range(T - TB, TC - 1, -TB)]
            for (b0, rev) in blocks:
                vb = VB[nbk % 2]
                nbk += 1
                S.dma("sp", XB[:], kq_tm[b0:b0 + TB, :].rearrange("(o t) n -> o t n", o=1), reads=["dn_kq_tm"], writes=[XB])
                S.dma("pool", GB[:], gb_tm[d][b0:b0 + TB, :].rearrange("(o t) n -> o t n", o=1), reads=[gb_tm[d].tensor.name], writes=[GB])
                S.dma("sp", vb[:], vF[:, b0:b0 + TB].rearrange("(h p) t -> p h t", p=128), reads=["dn_vF"], writes=[vb],
                      allow_slow_non_contiguous=True)
                for j in (range(TB - 1, -1, -1) if rev else range(TB)):
                    t = b0 + j
                    base = 0
                    bs = []
                    for qi in range(nq):
                        bank = self.banks[qi % 4]
                        n = min(512, 2 * HW - qi * 512)
                        S.op("pe", lambda e, qi=qi, n=n, bank=bank, j=j: e.matmul(
                            bank[:, 0:n], lhsT=ones1[:], rhs=XB[0:1, j, qi * 512:qi * 512 + n], start=True, stop=True),
                            reads=[ones1, XB], writes=[bank])
                        bs.append(bank)
                    gbank = self.banks[6 + stp % 2]
                    stp += 1
                    S.op("pe", lambda e, j=j, gbank=gbank: e.matmul(gbank[:, 0:2 * H], lhsT=ones1[:], rhs=GB[0:1, j, :], start=True, stop=True),
                         reads=[ones1, GB], writes=[gbank])

                    def seg(a, h):
                        off = a * HW + h * 128
                        return bs[off // 512][:, off % 512:off % 512 + 128]
                    for h in range(H):
                        S.op("dve", lambda e, h=h: e.tensor_tensor(out=tmp[:, h, :], in0=St[:, h, :], in1=seg(0, h), op=ALU.mult),
                             reads=[St, bs[(h * 128) // 512]], writes=[(tmp, h)])
                    S.op("dve", lambda e: e.tensor_reduce(out=sk[:], in_=tmp[:], axis=AX.X, op=ALU.add),
                         reads=[(tmp, h) for h in range(H)], writes=[sk])
                    S.op("dve", lambda e, gbank=gbank: e.tensor_tensor(out=dd[:], in0=sk[:], in1=gbank[:, 0:H], op=ALU.mult),
                         reads=[sk, gbank], writes=[dd])
                    S.op("dve", lambda e, vb=vb, j=j: e.tensor_tensor(out=dd[:], in0=vb[:, :, j], in1=dd[:], op=ALU.subtract),
                         reads=[dd, vb], writes=[dd])
                    S.op("dve", lambda e, gbank=gbank: e.tensor_tensor(out=dd[:], in0=dd[:], in1=gbank[:, H:2 * H], op=ALU.mult),
                         reads=[dd, gbank], writes=[dd])
                    S.op("dve", lambda e, gbank=gbank: e.tensor_tensor(
                        out=St[:], in0=St[:], in1=gbank[:, 0:H].unsqueeze(2).to_broadcast([128, H, 128]), op=ALU.mult),
                        reads=[St, gbank], writes=[St])
                    for h in range(H):
                        S.op("dve", lambda e, h=h: e.scalar_tensor_tensor(
                            out=St[:, h, :], in0=seg(0, h), scalar=dd[:, h:h + 1], in1=St[:, h, :], op0=ALU.mult, op1=ALU.add),
                            reads=[St, dd, bs[(h * 128) // 512]], writes=[St])
                    for h in range(H):
                        S.op("dve", lambda e, h=h: e.tensor_tensor(out=tmp[:, h, :], in0=St[:, h, :], in1=seg(1, h), op=ALU.mult),
                             reads=[St, bs[(HW + h * 128) // 512]], writes=[(tmp, h)])
                    if d == 0:
                        S.op("dve", lambda e, t=t, d=d: e.tensor_reduce(out=O1[:, :, t], in_=tmp[:], axis=AX.X, op=ALU.add),
                             reads=[(tmp, h) for h in range(H)], writes=[("dnO", 0)])
                    else:
                        S.op("dve", lambda e: e.tensor_reduce(out=ot[:], in_=tmp[:], axis=AX.X, op=ALU.add),
                             reads=[(tmp, h) for h in range(H)], writes=[ot])
                        S.op("dve", lambda e, t=t: e.tensor_tensor(out=O1[:, :, t], in0=O1[:, :, t], in1=ot[:], op=ALU.add),
                             reads=[ot, ("dnO", 0)], writes=[("dnO", 0)])
        y = f("dny", [128, T]); m_ = f("dnm", [128, T]); sq = f("dnsq", [128, T])
        onesm = f("onesm", [128, 128])
        S.op("dve", lambda e: e.memset(onesm[:], 1.0 / 128), writes=[onesm])
        for h in range(H):
            S.op("dve", lambda e, h=h: e.tensor_copy(out=y[:], in_=O1[:, h, :]),
                 reads=[("dnO", 0)], writes=[y])
            S.op("pool", lambda e: e.tensor_tensor(out=sq[:], in0=y[:], in1=y[:], op=ALU.mult), reads=[y], writes=[sq])
            self.headsum(m_, sq, T, onesm)
            self.rsqrt_inplace(m_[:], m_, cfg.EPS)
            S.op("dve", lambda e: e.scalar_tensor_tensor(out=y[:], in0=y[:], scalar=nwc[:, 0:1], in1=m_[:], op0=ALU.mult, op1=ALU.mult),
                 reads=[y, nwc, m_], writes=[y])
            S.dma("sp", sq[:], zF[h * 128:(h + 1) * 128, :], reads=["dn_zF"], writes=[sq])
            S.op("dve", lambda e, h=h: e.tensor_tensor(out=self.catT[:, h, :], in0=y[:], in1=sq[:], op=ALU.mult),
                 reads=[y, sq], writes=[("catT", h)])
        self.scope_end()
        self.scope_begin()
        self.out_proj(self.catT, DC, A["dn_w_out"], 1, 2, True, "d")
        self.scope_end()
        self.scope_end()


def host_consts(cfg):
    sel2 = np.zeros((2, 256), np.float32)
    sel2[0, 0:128] = 1.0
    sel2[1, 128:256] = 1.0
    blk = np.zeros((128, 128), np.float32)
    blk[0:64, 0:64] = 1.0
    blk[64:128, 64:128] = 1.0
    selH = np.zeros((2, 128), np.float32)
    selH[0, 0:64] = 1.0
    selH[1, 64:128] = 1.0
    out = {"ident": np.eye(128, dtype=np.float32), "sel2": sel2, "blk64": blk, "selH": selH}
    HY = cfg.HY
    deltas = np.abs(np.linspace(math.log(1e-2) / 1.5, math.log(1e-2) / 0.3, HY, dtype=np.float32))
    out["hy_negdelta"] = (-np.tile(deltas, 2)).astype(np.float32)
    for nm, L in (("ctx", cfg.TC), ("lat", cfg.TL)):
        t = np.linspace(0.0, 1.0, L, dtype=np.float32)[:, None]
        w = (2 * math.pi * np.arange(L, dtype=np.float32)[:, None] / L).astype(np.float32)
        fq = np.linspace(1e-4, 15, 16, dtype=np.float32)[None, :]
        z = np.concatenate([t, np.cos(fq * w), -np.sin(fq * w)], axis=-1).astype(np.float32)
        out["hy_z_" + nm] = np.ascontiguousarray(z.T)
        out["hy_t_" + nm] = np.ascontiguousarray(t[:, 0])
    return out


def make_in_maps(cfg, inputs, n_cores):
    consts = host_consts(cfg)
    maps = []
    f = lambda a: np.ascontiguousarray(np.asarray(a, dtype=np.float32))
    for b in range(n_cores):
        m = dict(consts)
        m["x"] = f(inputs["x"][b])
        m["ctx"] = f(inputs["ctx"][b])
        m["c2"] = f(np.stack([np.asarray(inputs["c"][b]), np.asarray(inputs["c_ctx"])]))
        for k in ("ada_w", "ada_b", "norm_mix", "norm_mlp", "mlp_w1", "mlp_w2", "final_norm"):
            m[k] = f(inputs[k])
        for k in ("ab_w_in", "ab_w_out", "rw_mu", "rw_w0", "rw_a0", "rw_w_up", "rw_a_up", "rw_g_up", "rw_k_k", "rw_k_a",
                  "rw_ln_w", "rw_ln_b", "hy_conv_w", "hy_conv_b", "hy_f_w1", "hy_f_w2", "hy_f_w3", "hy_f_w4",
                  "hy_f_b1", "hy_f_b2", "hy_f_b3", "hy_freq", "hy_skip", "dn_w_in", "dn_conv_w", "dn_A_log",
                  "dn_dt_bias", "dn_norm", "dn_w_out"):
            m[k] = f(np.asarray(inputs[k])[0])
        m["rw_r_k"] = f(np.asarray(inputs["rw_r_k"])[0].reshape(-1))
        maps.append(m)
    return maps


def kernel(**inputs):
    cfg = Cfg()
    b = Builder(cfg)
    nc = b.build()
    maps = make_in_maps(cfg, inputs, 8)
    maps = [{k: v for k, v in m.items() if k in b.inp} for m in maps]
    res = run_bass_kernel_spmd(nc, maps, core_ids=list(range(8)))
    return np.stack([np.asarray(r["out"], dtype=np.float32) for r in res.results], axis=0)
```
